# Optimizing a Trainium2 kernel written in Bass

```python
import jax, jax.numpy as jnp
from jax import lax
import numpy as np

D_MODEL = 1024
BATCH = 16
SEQ = 2048
DEPTH = 1

HEAD_DIM = 64
ATTN_HEADS = 8
ATTN_WIDTH = ATTN_HEADS * HEAD_DIM
ROPE_DIM = HEAD_DIM // 4
ROPE_THETA = 500000.0
DILATED_PATTERNS = ((128, 1), (512, 4), (2048, 16))
HG_HEADS = 4
HG_EXPAND = 128
HG_VDIM = 128
HG_KDIM = HG_HEADS * HG_EXPAND
HG_WIDTH = HG_HEADS * HG_VDIM
HG_CHUNK = 64
MIX_WIDTH = ATTN_WIDTH + HG_WIDTH
IN_SIZES = (ATTN_WIDTH, ATTN_WIDTH, ATTN_WIDTH, HG_KDIM, HG_KDIM, HG_WIDTH, HG_WIDTH)
IN_COLS = sum(IN_SIZES)
D_FF = 2816
EPS = 1e-6
NEG_INF = -1e30

kernel_name = 'hymba_longnet_hgrn2_macaron_block'


def _rms_norm(x, w):
    xf = x.astype(jnp.float32)
    y = xf * lax.rsqrt(jnp.mean(xf * xf, axis=-1, keepdims=True) + EPS)
    return (y * w.astype(jnp.float32)).astype(x.dtype)


def _swiglu(h, w1, w3, w2):
    return (jax.nn.silu(h @ w1) * (h @ w3)) @ w2


def _rope_tables(s):
    inv = ROPE_THETA ** (-jnp.arange(0, ROPE_DIM, 2, dtype=jnp.float32) / ROPE_DIM)
    ang = jnp.arange(s, dtype=jnp.float32)[:, None] * inv[None, :]
    return jnp.cos(ang), jnp.sin(ang)


def _partial_rope(x, cos, sin):
    half = ROPE_DIM // 2
    xf = x.astype(jnp.float32)
    x1, x2, rest = xf[..., :half], xf[..., half:ROPE_DIM], xf[..., ROPE_DIM:]
    c, s_ = cos[None, :, None, :], sin[None, :, None, :]
    out = jnp.concatenate([x1 * c - x2 * s_, x1 * s_ + x2 * c, rest], axis=-1)
    return out.astype(x.dtype)


def _dilated_branch(q, k, v, window, dilation):
    b, s, h, dh = q.shape
    w = window // dilation
    l = s // dilation
    nb = -(-l // w)
    lp = nb * w
    bb = b * dilation

    def to_sub(t):
        t = t.reshape(b, l, dilation, h, dh).transpose(0, 2, 1, 3, 4).reshape(bb, l, h, dh)
        return jnp.pad(t, ((0, 0), (0, lp - l), (0, 0), (0, 0)))

    def windows(t):
        t = jnp.pad(t, ((0, 0), (w, 0), (0, 0), (0, 0))).reshape(bb, nb + 1, w, h, dh)
        return jnp.concatenate([t[:, :-1], t[:, 1:]], axis=2)

    qb = to_sub(q).reshape(bb, nb, w, h, dh)
    kw, vw = windows(to_sub(k)), windows(to_sub(v))
    scores = jnp.einsum('bnqhd,bnkhd->bnhqk', qb, kw,
                        preferred_element_type=jnp.float32) * (dh ** -0.5)
    qi = jnp.arange(w)[:, None]
    kj = jnp.arange(2 * w)[None, :]
    dist = qi + w - kj
    band = (dist >= 0) & (dist <= w)
    valid = band[None] & ((jnp.arange(nb)[:, None, None] > 0) | (kj >= w)[None])
    scores = jnp.where(valid[None, :, None], scores, NEG_INF)
    m = jnp.max(scores, axis=-1, keepdims=True)
    p = jnp.exp(scores - m)
    den = jnp.sum(p, axis=-1, keepdims=True)
    out = jnp.einsum('bnhqk,bnkhd->bnqhd', p / den, vw.astype(jnp.float32))
    lse = (m + jnp.log(den))[..., 0]
    out = (out.reshape(bb, lp, h, dh)[:, :l]
           .reshape(b, dilation, l, h, dh).transpose(0, 2, 1, 3, 4).reshape(b, s, h, dh))
    lse = (lse.transpose(0, 1, 3, 2).reshape(bb, lp, h)[:, :l]
           .reshape(b, dilation, l, h).transpose(0, 2, 1, 3).reshape(b, s, h))
    return out, lse


def _dilated_attention(q, k, v):
    outs, lses = [], []
    for window, dilation in DILATED_PATTERNS:
        o, lse = _dilated_branch(q, k, v, window, dilation)
        outs.append(o)
        lses.append(lse)
    wts = jax.nn.softmax(jnp.stack(lses), axis=0)
    return jnp.einsum('pbsh,pbshd->bshd', wts, jnp.stack(outs))


def _hgrn2(q, f_pre, inp, lb):
    b, s, h, n = q.shape
    dv = inp.shape[-1]
    c = HG_CHUNK
    nc = s // c
    lb = lb.reshape(h, n).astype(jnp.float32)
    f = lb + (1.0 - lb) * jax.nn.sigmoid(f_pre.astype(jnp.float32))
    g = jnp.log(f)
    k = 1.0 - f

    def chunks(t):
        return t.reshape(b, nc, c, h, t.shape[-1]).transpose(0, 3, 1, 2, 4)

    qc, kc, gc, vc = chunks(q.astype(jnp.float32)), chunks(k), chunks(g), chunks(inp.astype(jnp.float32))
    G = jnp.cumsum(gc, axis=3)
    G_last = G[:, :, :, -1:]
    q_dec = qc * jnp.exp(G)
    att = jnp.einsum('bhntk,bhnsk->bhnts', q_dec, kc * jnp.exp(-G))
    att = jnp.where(jnp.tril(jnp.ones((c, c), dtype=bool)), att, 0.0)
    intra = jnp.einsum('bhnts,bhnsv->bhntv', att, vc)
    chunk_state = jnp.einsum('bhnsk,bhnsv->bhnkv', kc * jnp.exp(G_last - G), vc)
    decay = jnp.exp(G_last[:, :, :, 0])

    def step(state, xs):
        dec, upd = xs
        return dec[..., None] * state + upd, state

    _, prev = lax.scan(step, jnp.zeros((b, h, n, dv), jnp.float32),
                       (jnp.moveaxis(decay, 2, 0), jnp.moveaxis(chunk_state, 2, 0)))
    prev = jnp.moveaxis(prev, 0, 2)
    inter = jnp.einsum('bhntk,bhnkv->bhntv', q_dec, prev)
    return (intra + inter).transpose(0, 2, 3, 1, 4).reshape(b, s, h, dv)


def setup_inputs(seed: int = 0) -> dict:
    key = jax.random.key(seed)
    ks = jax.random.split(key, 20)
    f32 = jnp.float32

    def nrm(k, shape, fan_in):
        return jax.random.normal(k, shape, f32) * (fan_in ** -0.5)

    def gain(k, shape):
        return 1.0 + 0.02 * jax.random.normal(k, shape, f32)

    return {
        'x': jax.random.normal(ks[0], (BATCH, SEQ, D_MODEL), f32),
        'ffn1_norm': gain(ks[1], (DEPTH, D_MODEL)),
        'ffn1_w1': nrm(ks[2], (DEPTH, D_MODEL, D_FF), D_MODEL),
        'ffn1_w3': nrm(ks[3], (DEPTH, D_MODEL, D_FF), D_MODEL),
        'ffn1_w2': nrm(ks[4], (DEPTH, D_FF, D_MODEL), D_FF),
        'mix_norm': gain(ks[5], (DEPTH, D_MODEL)),
        'w_in': nrm(ks[6], (DEPTH, D_MODEL, IN_COLS), D_MODEL),
        'q_norm': gain(ks[7], (DEPTH, HEAD_DIM)),
        'k_norm': gain(ks[8], (DEPTH, HEAD_DIM)),
        'hg_lb_logits': 0.1 * jax.random.normal(ks[9], (DEPTH + 1, HG_KDIM), f32),
        'hg_out_norm': gain(ks[10], (DEPTH, HG_VDIM)),
        'w_out': nrm(ks[11], (DEPTH, MIX_WIDTH, D_MODEL), MIX_WIDTH),
        'ffn2_norm': gain(ks[12], (DEPTH, D_MODEL)),
        'ffn2_w1': nrm(ks[13], (DEPTH, D_MODEL, D_FF), D_MODEL),
        'ffn2_w3': nrm(ks[14], (DEPTH, D_MODEL, D_FF), D_MODEL),
        'ffn2_w2': nrm(ks[15], (DEPTH, D_FF, D_MODEL), D_FF),
    }


def reference(x, ffn1_norm, ffn1_w1, ffn1_w3, ffn1_w2, mix_norm, w_in, q_norm, k_norm,
              hg_lb_logits, hg_out_norm, w_out, ffn2_norm, ffn2_w1, ffn2_w3, ffn2_w2):
    b, s, _ = x.shape
    cos, sin = _rope_tables(s)
    lower_bounds = jnp.cumsum(jax.nn.softmax(hg_lb_logits.astype(jnp.float32), axis=0), axis=0)
    split_at = list(np.cumsum(IN_SIZES)[:-1])
    for layer in range(DEPTH):
        x = x + 0.5 * _swiglu(_rms_norm(x, ffn1_norm[layer]),
                              ffn1_w1[layer], ffn1_w3[layer], ffn1_w2[layer])
        h = _rms_norm(x, mix_norm[layer])
        proj = h @ w_in[layer]
        aq, ak, av, hq, hf, hi, hg = jnp.split(proj, split_at, axis=-1)
        aq = _partial_rope(_rms_norm(aq.reshape(b, s, ATTN_HEADS, HEAD_DIM), q_norm[layer]), cos, sin)
        ak = _partial_rope(_rms_norm(ak.reshape(b, s, ATTN_HEADS, HEAD_DIM), k_norm[layer]), cos, sin)
        av = av.reshape(b, s, ATTN_HEADS, HEAD_DIM)
        attn_out = _dilated_attention(aq, ak, av).astype(x.dtype).reshape(b, s, ATTN_WIDTH)
        rec = _hgrn2(hq.reshape(b, s, HG_HEADS, HG_EXPAND), hf.reshape(b, s, HG_HEADS, HG_EXPAND),
                     hi.reshape(b, s, HG_HEADS, HG_VDIM), lower_bounds[layer]).astype(x.dtype)
        rec = _rms_norm(rec, hg_out_norm[layer]) * jax.nn.silu(hg.reshape(b, s, HG_HEADS, HG_VDIM))
        mixed = jnp.concatenate([attn_out, rec.reshape(b, s, HG_WIDTH)], axis=-1)
        x = x + mixed @ w_out[layer]
        x = x + 0.5 * _swiglu(_rms_norm(x, ffn2_norm[layer]),
                              ffn2_w1[layer], ffn2_w3[layer], ffn2_w2[layer])
    return x
```

```python
import os
import numpy as np
import ml_dtypes
from contextlib import ExitStack
import concourse.bass as bass
import concourse.mybir as mybir
from concourse.bass_utils import run_bass_kernel_spmd

F32 = mybir.dt.float32
BF16 = mybir.dt.bfloat16
AF = mybir.ActivationFunctionType
ALU = mybir.AluOpType
AX = mybir.AxisListType
EPS = 1e-6

NCORES = 8
S = 2048
D = 1024
DFF = 2816
NFC = 22
TOK = 2 * S

GAM, COS, SIN, GQK, LBL, GOUT, ONESF, SCM = 0, 24, 152, 280, 792, 800, 801, 929
NCF = 996
IDB, MSK, HMK, ONEB = 0, 128, 384, 512
NCB = 640

ARENA_BYTES = 105472

ENGS = ("pe", "act", "dve", "pool", "sp")


class Op:
    __slots__ = ("eng", "fn", "dma", "deps", "flag", "cnt", "dsem", "dval", "idx", "tend")

    def __init__(self, eng, fn, dma):
        self.eng = eng
        self.fn = fn
        self.dma = dma
        self.deps = []
        self.flag = False
        self.cnt = 0
        self.dsem = None
        self.dval = 0


class Prog:
    def __init__(self, nc, ndma_sems=16):
        self.nc = nc
        self.ops = []
        self.last_w = {}
        self.readers = {}
        self.ndma = ndma_sems
        self.pending = {}
        self.pending_dma = []
        self.eng_free = {e: 0.0 for e in ENGS}
        self.last_end = 0.0

    def retire(self, names):
        names = set(names)
        dead = [k for k in list(self.last_w.keys()) + list(self.readers.keys())
                if (k[0] if isinstance(k, tuple) else k) in names]
        for k in set(dead):
            ops = []
            w = self.last_w.pop(k, None)
            if w is not None:
                ops.append(w)
            ops.extend(self.readers.pop(k, ()))
            for o in ops:
                if o.dma:
                    if o not in self.pending_dma:
                        self.pending_dma.append(o)
                else:
                    cur = self.pending.get(o.eng)
                    if cur is None or cur.idx < o.idx:
                        self.pending[o.eng] = o

    def op(self, eng, fn, reads=(), writes=(), dma=False, dur=None, aset=None):
        o = Op(eng, fn, dma)
        o.idx = len(self.ops)
        if dur is None:
            dur = 2.0 if dma else (0.12 if eng == "pe" else (0.63 if eng == "act" else 0.7))
        if aset is not None:
            if getattr(self, "cur_aset", None) not in (None, aset):
                dur += 1.3
            self.cur_aset = aset
        deps = set()
        for k in reads:
            w = self.last_w.get(k)
            if w is not None:
                deps.add(w)
        for k in writes:
            if k not in self.last_w and k not in self.readers:
                deps.update(self.pending.values())
                deps.update(self.pending_dma)
            w = self.last_w.get(k)
            if w is not None:
                deps.add(w)
            for r in self.readers.get(k, ()):
                deps.add(r)
        for k in writes:
            self.last_w[k] = o
            self.readers[k] = []
        for k in reads:
            self.readers.setdefault(k, []).append(o)
        deps.discard(o)
        for d in deps:
            if (not d.dma) and (not o.dma) and d.eng == o.eng and o.eng == "pe":
                continue
            o.deps.append(d)
        t0 = self.eng_free[eng] if not dma else 0.0
        for d in deps:
            if d.tend + 0.1 > t0:
                t0 = d.tend + 0.1
        o.tend = t0 + dur
        if not dma:
            self.eng_free[eng] = o.tend
        self.last_end = o.tend
        self.ops.append(o)
        return o

    def emit(self, stack):
        nc = self.nc
        per = {e: [o for o in self.ops if o.eng == e] for e in ENGS}
        for o in self.ops:
            for d in o.deps:
                if not d.dma:
                    d.flag = True
        esem = {e: stack.enter_context(nc.semaphore("s_" + e)) for e in ENGS if e != "sp"}
        for e in ENGS:
            c = 0
            for o in per[e]:
                if not o.dma and o.flag:
                    c += 1
                    o.cnt = c
        extra_wait = {}
        for e in ENGS:
            dl = [o for o in per[e] if o.dma]
            if not dl:
                continue
            sems = [stack.enter_context(nc.semaphore("d_%s%d" % (e, i))) for i in range(self.ndma)]
            for j, o in enumerate(dl):
                o.dsem = sems[j % self.ndma]
                o.dval = 16 * (j // self.ndma + 1)
                if j >= self.ndma:
                    extra_wait[o] = dl[j - self.ndma]
        self.max_cnt = {e: max([o.cnt for o in per[e]] + [0]) for e in ENGS}
        block = stack.enter_context(nc.Block())
        handles = {"pe": block.tensor, "act": block.scalar, "dve": block.vector,
                   "pool": block.gpsimd, "sp": block.sync}

        def run(e, eng):
            waited = {}
            for o in per[e]:
                deps = list(o.deps)
                if o in extra_wait:
                    deps.append(extra_wait[o])
                need = {}
                for d in deps:
                    if d.dma:
                        s, v = d.dsem, d.dval
                    else:
                        s, v = esem[d.eng], d.cnt
                    key = id(s)
                    if waited.get(key, 0) >= v:
                        continue
                    if key not in need or need[key][1] < v:
                        need[key] = (s, v)
                for key, (s, v) in need.items():
                    eng.wait_ge(s, v)
                    waited[key] = v
                ins = o.fn(eng)
                if o.dma:
                    ins.then_inc(o.dsem, 16)
                elif o.flag:
                    ins.then_inc(esem[e], 1)
            last = {}
            for o in per[e]:
                if o.dma:
                    last[id(o.dsem)] = (o.dsem, o.dval)
            for key, (s, v) in last.items():
                if waited.get(key, 0) < v:
                    eng.wait_ge(s, v)

        for e in ENGS:
            if per[e]:
                handles[e]((lambda ee: (lambda eng: run(ee, eng)))(e))


class Rot:
    def __init__(self, items):
        self.items = list(items)
        self.i = 0

    def next(self):
        it = self.items[self.i % len(self.items)]
        self.i += 1
        return it


def build(stage=3, dbg=False, nseq=2):
    nc = bass.Bass("TRN2", target_bir_lowering=False)

    def din(name, shape, dt=F32):
        return nc.dram_tensor(name, shape, dt, kind="ExternalInput").ap()

    x_d = din("x", [TOK, D])
    out_d = nc.dram_tensor("out", [TOK, D], F32, kind="ExternalOutput").ap()
    w1a = din("ffn1_w1", [D, DFF]); w3a = din("ffn1_w3", [D, DFF]); w2a = din("ffn1_w2", [DFF, D])
    w1b_ = din("ffn2_w1", [D, DFF]); w3b_ = din("ffn2_w3", [D, DFF]); w2b_ = din("ffn2_w2", [DFF, D])
    win_d = din("w_in", [D, 3584]); wout_d = din("w_out", [D, D])
    cf_d = din("cf", [128, NCF]); cb_d = din("cb", [128, NCB], BF16)
    if dbg:
        dbg_mixed = nc.dram_tensor("dbg_mixed", [128, 8, S], BF16, kind="ExternalOutput").ap()
        dbg_hT = nc.dram_tensor("dbg_hT", [128, 8, S], BF16, kind="ExternalOutput").ap()

    x_v = x_d.rearrange("(t p) d -> p t d", p=128)
    out_v = out_d.rearrange("(t p) d -> p t d", p=128)
    kv = lambda w: w.rearrange("(k p) n -> p k n", p=128)
    win_v = kv(win_d)
    wout_v = kv(wout_d)

    with ExitStack() as st:
        sb = lambda name, shape, dt: st.enter_context(nc.sbuf_tensor(name, shape, dt))
        resid = sb("resid", [128, 16, D], F32)
        hTb = sb("hTb", [128, 16384], BF16)
        cf = sb("cf_sb", [128, NCF], F32)
        cb = sb("cb_sb", [128, NCB], BF16)
        ms = sb("ms", [128, 96], F32)
        rs = sb("rs", [128, 96], F32)
        lbt = sb("lbt", [128, 8], F32)
        ss8 = sb("ss8", [128, 4, 8], F32)
        rs8 = sb("rs8", [128, 4, 8], F32)
        Sf = sb("Sf", [128, 2, 128], F32)
        arena = sb("arena", [128, ARENA_BYTES // 2], BF16)
        psf = [st.enter_context(nc.psum_tensor("ps%d" % i, [128, 512], F32)) for i in range(8)]
        psb = [psf[6 + i][:, :].bitcast(BF16).rearrange("p (k t) -> p k t", t=128) for i in range(2)]

        def carve(off, shape, dt):
            n = int(np.prod(shape[1:]))
            nb = n * (4 if dt == F32 else 2)
            assert off % 4 == 0 and off + nb <= ARENA_BYTES, (off, shape)
            a = arena[:, off // 2:(off + nb) // 2]
            if dt == F32:
                a = a.bitcast(F32)
            if len(shape) == 3:
                a = a.rearrange("p (a b) -> p a b", b=shape[2])
            elif len(shape) == 4:
                a = a.rearrange("p (a b c) -> p a b c", b=shape[2], c=shape[3])
            elif len(shape) == 5:
                a = a.rearrange("p (a b c d) -> p a b c d", b=shape[2], c=shape[3], d=shape[4])
            return a

        hTm = hTb.rearrange("p (k t) -> p k t", t=2048)
        hTf = hTb[:, 0:8192].rearrange("p (k t) -> p k t", t=1024)
        w13 = [hTb[:, 8192 + i * 2048: 8192 + (i + 1) * 2048].rearrange("p (k n) -> p k n", n=256) for i in range(4)]
        w1s, w3s = w13[0:2], w13[2:4]

        gT = carve(0, [128, NFC, 1024], BF16)
        w2sb = carve(45056, [128, NFC, 1024], BF16)
        sg = [carve(90112 + i * 2048, [128, 512], F32) for i in range(2)]
        xn_f = [carve(94208 + i * 2048, [128, 1024], BF16) for i in range(4)]
        mixedT = carve(0, [128, 8, 2048], BF16)
        A0 = 32768
        xn_m = [carve(A0 + i * 2048, [128, 1024], BF16) for i in range(2)]
        o = A0
        wqk = carve(o, [128, 8, 512], BF16); o += 8192
        wv = carve(o, [128, 8, 256], BF16); o += 4096
        qkT = carve(o, [128, 4, 2048], BF16); o += 16384
        Vaug = []
        for b in range(3):
            Vaug.append(carve(o, [128, 16, 2, 192], BF16)); o += 12288
        rden = carve(o, [128, 512], F32); o += 2048
        o += 2048
        pt = [carve(A0 + i * 1024, [128, 512], BF16) for i in range(6)]
        VA2 = A0 + 8192 + 4096 + 16384 + 2 * 12288
        qn2 = [carve(VA2 + i * 2048, [128, 512], F32) for i in range(2)]
        qr2 = [carve(VA2 + 4096 + i * 1024, [128, 512], BF16) for i in range(2)]
        rt2 = [carve(VA2 + 6144 + i * 1024, [128, 4, 64], F32) for i in range(2)]
        qn_a = carve(o, [128, 512], F32); o += 2048
        sq_a = qn_a
        qr_a = carve(o, [128, 512], BF16); o += 1024
        assert o <= ARENA_BYTES, o
        o = A0
        wi = carve(o, [128, 8, 512], BF16); o += 8192
        wqfg2 = []
        for i in range(2):
            wqfg2.append(carve(o, [128, 8, 3, 128], BF16)); o += 6144
        vtm = carve(o, [128, 16, 512], BF16); o += 16384
        hA, hB, hC = [], [], []
        for i in range(2):
            hA.append(carve(o, [128, 512], F32)); o += 2048
            hB.append(carve(o, [128, 512], F32)); o += 2048
            hC.append(carve(o, [128, 512], F32)); o += 2048
        qd, kd, kk, kktm, Sbf, silg = [], [], [], [], [], []
        for i in range(2):
            qd.append(carve(o, [128, 512], BF16)); o += 1024
            kd.append(carve(o, [128, 512], BF16)); o += 1024
            if i == 0:
                kk.append(carve(o, [128, 512], BF16)); o += 1024
            else:
                kk.append(kk[0])
            kktm.append(carve(o, [128, 4, 128], BF16)); o += 1024
            Sbf.append(carve(o, [128, 8, 128], BF16)); o += 2048
            silg.append(carve(o, [128, 512], BF16)); o += 1024
        silg.append(carve(o, [128, 512], BF16)); o += 1024
        hsq = carve(o, [128, 512], F32); o += 2048
        hrstd = carve(o, [128, 512], F32); o += 2048
        attm = []
        for i in range(4):
            attm.append(carve(o, [128, 128], BF16)); o += 256
        hscm = carve(o, [128, 512], F32); o += 2048
        hhi = carve(o, [128, 512], BF16); o += 1024
        hlo = carve(o, [128, 512], BF16); o += 1024
        assert o <= ARENA_BYTES, o
        woA = carve(A0, [128, 8, 512], BF16)
        woB = carve(A0 + 8192, [128, 8, 512], BF16)

        identb = cb[:, IDB:IDB + 128]
        amask = cb[:, MSK:MSK + 256]
        hmask = cb[:, HMK:HMK + 128]

        p = Prog(nc)
        PSK = lambda b, h=None: [("ps", b)]

        p.op("sp", lambda e: e.dma_start(out=cf[:, :], in_=cf_d[:, :]), writes=["cf"], dma=True)
        p.op("sp", lambda e: e.dma_start(out=cb[:, :], in_=cb_d[:, :]), writes=["cb"], dma=True)
        p.op("dve", lambda e: e.memset(ms[:, :], 0.0), writes=["ms_all"])
        p.op("dve", lambda e: e.tensor_tensor(out=lbt[:, 0:4], in0=cf[:, LBL:LBL + 4], in1=cf[:, LBL + 4:LBL + 8], op=ALU.subtract),
             reads=["cf"], writes=["lbt0"])
        p.op("dve", lambda e: e.tensor_tensor(out=lbt[:, 4:8], in0=cf[:, LBL + 4:LBL + 8], in1=cf[:, LBL:LBL + 4], op=ALU.subtract),
             reads=["cf"], writes=["lbt1"])
        p.op("act", lambda e: e.activation(out=lbt[:, :], in_=lbt[:, :], func=AF.Sigmoid), reads=["lbt0", "lbt1"], writes=["lbt"])

        nidx = [0]
        rot_T = Rot([0, 1])

        def norm_tile(tt, dst, dst_key, gidx, xn, defer=False):
            i = nidx[0]
            nidx[0] += 1
            slot = i % len(xn)
            xs = xn[slot]
            p.op("act", lambda e: e.activation(out=xs[:, :], in_=resid[:, tt, :], func=AF.Square, scale=1.0 / 32.0,
                                               accum_out=ms[:, i:i + 1]),
                 reads=[("res", tt), "ms_all"], writes=[("xn", slot), ("ms", i)])
            p.op("act", lambda e: e.activation(out=rs[:, i:i + 1], in_=ms[:, i:i + 1], func=AF.Sqrt, bias=EPS, scale=1.0),
                 reads=[("ms", i)], writes=[("rs", i)])
            p.op("dve", lambda e: e.reciprocal(out=rs[:, i:i + 1], in_=rs[:, i:i + 1]), reads=[("rs", i)], writes=[("rs", i)])
            p.op("dve", lambda e: e.tensor_scalar(out=xs[:, :], in0=resid[:, tt, :], scalar1=rs[:, i:i + 1], scalar2=None, op0=ALU.mult),
                 reads=[("res", tt), ("rs", i)], writes=[("xn", slot)])
            def part_b():
                tb = rot_T.next()
                for kc in range(8):
                    p.op("pe", lambda e, kc=kc: e.transpose(out=psb[tb][:, kc, :], in_=xs[:, kc * 128:(kc + 1) * 128], identity=identb),
                         reads=[("xn", slot), "cb"], writes=[("ps", 6 + tb)])
                gb = cf[:, GAM + gidx * 8:GAM + gidx * 8 + 8].unsqueeze(2).to_broadcast([128, 8, 128])
                p.op("dve", lambda e: e.tensor_tensor(out=dst, in0=psb[tb][:, :, :], in1=gb, op=ALU.mult),
                     reads=[("ps", 6 + tb), "cf"], writes=[dst_key])
            if defer:
                return part_b
            part_b()

        rot_h = Rot([0, 1, 2, 3])
        rot_o = Rot([(0, 1), (2, 3), (4, 5)])
        sgi = [0]

        def ffn(seq, half, w1d, w3d, w2d, gidx, store, do_norm=True, norm_next=False, mix_norm=False, next_seq_norm=False):
            tt0 = half * 8
            w1v, w3v = kv(w1d), kv(w3d)
            w2v = w2d.rearrange("(c p) d -> p c d", p=128)
            if do_norm:
                pq = []
                for t in range(8 + 3):
                    if t < 8:
                        pq.append(norm_tile(tt0 + t, hTf[:, :, t * 128:(t + 1) * 128], ("hTf", t), gidx, xn_f, defer=True))
                    if t >= 3:
                        pq.pop(0)()
            hkeys = [[("hTf", t) for t in range(th * 4, th * 4 + 4)] for th in range(2)]
            for cg in range(11):
                slot = cg % 2
                p.op("pool", lambda e, cg=cg, slot=slot: e.dma_start(out=w1s[slot][:, :, :], in_=w1v[:, :, cg * 256:(cg + 1) * 256]),
                     writes=[("w1b", slot)], dma=True)
                p.op("pool", lambda e, cg=cg, slot=slot: e.dma_start(out=w3s[slot][:, :, :], in_=w3v[:, :, cg * 256:(cg + 1) * 256]),
                     writes=[("w3b", slot)], dma=True)
                if cg in (1, 4):
                    pc = 0 if cg == 1 else 1
                    p.op("pool", lambda e, pc=pc: e.dma_start(out=w2sb[:, pc * 11:(pc + 1) * 11, :], in_=w2v[:, pc * 11:(pc + 1) * 11, :]),
                         writes=[("w2", pc)], dma=True)
                for cc in range(2):
                    c = cg * 2 + cc
                    for th in range(2):
                        b1 = rot_h.next()
                        b3 = rot_h.next()
                        for kc in range(8):
                            p.op("pe", lambda e, kc=kc, b1=b1, cc=cc, th=th, slot=slot: e.matmul(
                                psf[b1][:, :], lhsT=w1s[slot][:, kc, cc * 128:(cc + 1) * 128], rhs=hTf[:, kc, th * 512:(th + 1) * 512],
                                start=(kc == 0), stop=(kc == 7)),
                                reads=[("w1b", slot)] + hkeys[th], writes=PSK(b1))
                        for kc in range(8):
                            p.op("pe", lambda e, kc=kc, b3=b3, cc=cc, th=th, slot=slot: e.matmul(
                                psf[b3][:, :], lhsT=w3s[slot][:, kc, cc * 128:(cc + 1) * 128], rhs=hTf[:, kc, th * 512:(th + 1) * 512],
                                start=(kc == 0), stop=(kc == 7)),
                                reads=[("w3b", slot)] + hkeys[th], writes=PSK(b3))
                        si = sgi[0] % 2
                        sgi[0] += 1
                        p.op("act", lambda e, b1=b1, si=si: e.activation(out=sg[si][:, :], in_=psf[b1][:, :], func=AF.Silu),
                             reads=PSK(b1), writes=[("sg", si)])
                        p.op("dve", lambda e, b3=b3, si=si, c=c, th=th: e.tensor_tensor(
                            out=gT[:, c, th * 512:(th + 1) * 512], in0=sg[si][:, :], in1=psf[b3][:, :], op=ALU.mult),
                            reads=[("sg", si)] + PSK(b3), writes=[("gT", c, th)])
            if mix_norm:
                p.retire(["hTf", "w1b", "w3b"])
            due = {}
            for t in range(8):
                tt = tt0 + t
                for fb in due.pop(t, []):
                    fb()
                if norm_next:
                    due.setdefault(t + 1, []).append(norm_tile(tt0 + 8 + t, hTf[:, :, t * 128:(t + 1) * 128], ("hTf", t), gidx, xn_f, defer=True))
                if next_seq_norm:
                    due.setdefault(t + 1, []).append(norm_tile(t, hTf[:, :, t * 128:(t + 1) * 128], ("hTf", t), 0, xn_f, defer=True))
                if mix_norm:
                    due.setdefault(t + 1, []).append(norm_tile(t, hTm[:, :, t * 128:(t + 1) * 128], ("hTm", t), 1, xn_f, defer=True))
                banks = rot_o.next()
                for dh in range(2):
                    bk = banks[dh]
                    for c in range(NFC):
                        p.op("pe", lambda e, c=c, bk=bk, t=t, dh=dh: e.matmul(
                            psf[bk][:, :], lhsT=gT[:, c, t * 128:(t + 1) * 128], rhs=w2sb[:, c, dh * 512:(dh + 1) * 512],
                            start=(c == 0), stop=(c == NFC - 1)),
                            reads=[("gT", c, t // 4), ("w2", c // 11)], writes=PSK(bk))
                    p.op("dve", lambda e, bk=bk, tt=tt, dh=dh: e.scalar_tensor_tensor(
                        out=resid[:, tt, dh * 512:(dh + 1) * 512], in0=psf[bk][:, :], scalar=0.5,
                        in1=resid[:, tt, dh * 512:(dh + 1) * 512], op0=ALU.mult, op1=ALU.add),
                        reads=PSK(bk) + [("res", tt)], writes=[("res", tt)])
                if store:
                    g = seq * 16 + tt
                    p.op("sp", lambda e, g=g, tt=tt: e.dma_start(out=out_v[:, g, :], in_=resid[:, tt, :]),
                         reads=[("res", tt)], dma=True)
                if mix_norm:
                    tm = 8 + t
                    due.setdefault(t + 2, []).append(norm_tile(tm, hTm[:, :, tm * 128:(tm + 1) * 128], ("hTm", tm), 1, xn_f, defer=True))
            for k_ in sorted(due):
                for fb in due[k_]:
                    fb()

        FFN_NAMES = ["gT", "w2", "sg", "xn", "hTf", "w1b", "w3b"]
        ATT_NAMES = ["xn", "wqk", "wv", "qkT", "Va", "rden", "pt", "sq", "qn", "qr", "sqj", "rt"]
        HG_NAMES = ["wi", "wqfg", "vtm", "hA", "hB", "hC", "qd", "kd", "kk", "kktm", "Sbf", "silg", "hsq", "hrstd", "ht1", "attm", "hscm", "hhi", "hlo"]

        rot_full = Rot([0, 1, 2, 3, 4, 5])
        rot_half = Rot([(b, 0) for b in range(6)])
        rot_S = Rot([4, 5, 6, 7])
        pti = [0]
        evi = [0]

        def hslot(bh):
            b, h = bh
            return psf[b][:, h * 256:(h + 1) * 256]

        def attention_half(hf):
            allhT = [("hTm", tt) for tt in range(16)]
            def v_phase(bs):
              for b in bs:
                p.op("dve", lambda e, b=b: e.memset(Vaug[b][:, :, :, 64:128], 1.0), writes=[("Va", b, "ones")])
              for b in bs:
                Vv = Vaug[b].rearrange("p t a (k d) -> p t a k d", d=64)
                for tile in range(16):
                    if b == 0:
                        tok = lambda kc, tile=tile: hTm[:, kc, tile * 128:(tile + 1) * 128]
                        rk_ = [("hTm", tile)]
                    elif b == 1:
                        rho, j = tile // 4, tile % 4
                        s0 = 512 * j + rho
                        tok = lambda kc, s0=s0: hTm[:, kc, s0:s0 + 509:4]
                        rk_ = [("hTm", 4 * j + i) for i in range(4)]
                    else:
                        tok = lambda kc, tile=tile: hTm[:, kc, tile:tile + 2033:16]
                        rk_ = allhT
                    bh = rot_half.next()
                    for kc in range(8):
                        p.op("pe", lambda e, kc=kc, bh=bh, tok=tok: e.matmul(hslot(bh), lhsT=tok(kc), rhs=wv[:, kc, :], start=(kc == 0), stop=(kc == 7)),
                             reads=rk_ + [("wv", 0)], writes=PSK(*bh))
                    src = hslot(bh).rearrange("p (a k d) -> p a k d", a=2, k=2, d=64)
                    dst = Vv[:, tile, :, 0:3:2, :]
                    eng = "act" if evi[0] % 2 == 0 else "dve"
                    evi[0] += 1
                    if eng == "act":
                        p.op("act", lambda e, src=src, dst=dst: e.copy(out=dst, in_=src), reads=PSK(*bh), writes=[("Va", b, tile)])
                    else:
                        p.op("dve", lambda e, src=src, dst=dst: e.tensor_copy(out=dst, in_=src), reads=PSK(*bh), writes=[("Va", b, tile)])

            p.op("pool", lambda e: e.dma_start(out=wv[:, :, :], in_=win_v[:, :, 1024 + hf * 256:1024 + (hf + 1) * 256]), writes=[("wv", 0)], dma=True)
            p.op("pool", lambda e: e.dma_start(out=wqk[:, :, 0:256], in_=win_v[:, :, hf * 256:(hf + 1) * 256]), writes=[("wqk", 0)], dma=True)
            p.op("pool", lambda e: e.dma_start(out=wqk[:, :, 256:512], in_=win_v[:, :, 512 + hf * 256:512 + (hf + 1) * 256]), writes=[("wqk", 1)], dma=True)
            v_phase((0, 1))
            gq = cf[:, GQK:GQK + 512].rearrange("p (h d) -> p h d", d=64)
            qbank = {}

            def qk_mm(tt):
                bk = rot_full.next()
                qbank[tt] = bk
                for kc in range(8):
                    p.op("pe", lambda e, kc=kc: e.matmul(psf[bk][:, :], lhsT=hTm[:, kc, tt * 128:(tt + 1) * 128], rhs=wqk[:, kc, :],
                                                       start=(kc == 0), stop=(kc == 7)),
                         reads=[("hTm", tt), ("wqk", 0), ("wqk", 1)], writes=PSK(bk))

            def qk_Xa(tt):
                bk = qbank[tt]
                p.op("act", lambda e: e.activation(out=qn_a[:, :], in_=psf[bk][:, :], func=AF.Square, scale=0.125),
                     reads=PSK(bk), writes=["sqj"])

            def qk_Xb(tt):
                s4 = tt % 4
                p.op("dve", lambda e: e.tensor_reduce(out=ss8[:, s4, :], in_=qn_a.rearrange("p (h d) -> p h d", d=64), axis=AX.X, op=ALU.add),
                     reads=["sqj"], writes=[("ss8", s4)])
                p.op("act", lambda e: e.activation(out=rs8[:, s4, :], in_=ss8[:, s4, :], func=AF.Ln, bias=EPS, scale=1.0),
                     reads=[("ss8", s4)], writes=[("rs8", s4)])
                p.op("act", lambda e: e.activation(out=rs8[:, s4, :], in_=rs8[:, s4, :], func=AF.Exp, scale=-0.5),
                     reads=[("rs8", s4)], writes=[("rs8", s4)])

            def qk_Y(tt):
                bk = qbank[tt]
                sl = tt % 2
                qn_s, qr_s = qn2[sl], qr2[sl]
                ps3 = psf[bk].rearrange("p (h d) -> p h d", d=64)
                qn3 = qn_s.rearrange("p (h d) -> p h d", d=64)
                qr3 = qr_s.rearrange("p (h d) -> p h d", d=64)
                kqn = ("qn", sl)
                s4 = tt % 4
                p.op("dve", lambda e: e.tensor_tensor(out=qn3, in0=ps3, in1=rs8[:, s4, :].unsqueeze(2).to_broadcast([128, 8, 64]), op=ALU.mult),
                     reads=PSK(bk) + [("rs8", s4)], writes=[kqn])
                p.op("dve", lambda e: e.tensor_tensor(out=qn3, in0=qn3, in1=gq, op=ALU.mult), reads=[kqn, "cf"], writes=[kqn])
                cs_ = cf[:, COS + tt * 8:COS + tt * 8 + 8].unsqueeze(1).to_broadcast([128, 8, 8])
                sn_ = cf[:, SIN + tt * 8:SIN + tt * 8 + 8].unsqueeze(1).to_broadcast([128, 8, 8])
                x1 = qn3[:, :, 0:8]
                x2 = qn3[:, :, 8:16]
                T = [rt2[sl][:, i, :].rearrange("p (h d) -> p h d", d=8) for i in range(4)]
                rk = ("rt", sl)
                p.op("dve", lambda e: e.tensor_tensor(out=T[0], in0=x1, in1=cs_, op=ALU.mult), reads=[kqn, "cf"], writes=[rk + (0,)])
                p.op("dve", lambda e: e.tensor_tensor(out=T[1], in0=x2, in1=sn_, op=ALU.mult), reads=[kqn, "cf"], writes=[rk + (1,)])
                p.op("dve", lambda e: e.tensor_tensor(out=T[2], in0=x1, in1=sn_, op=ALU.mult), reads=[kqn, "cf"], writes=[rk + (2,)])
                p.op("dve", lambda e: e.tensor_tensor(out=T[3], in0=x2, in1=cs_, op=ALU.mult), reads=[kqn, "cf"], writes=[rk + (3,)])
                p.op("dve", lambda e: e.tensor_tensor(out=qr3[:, :, 0:8], in0=T[0], in1=T[1], op=ALU.subtract),
                     reads=[rk + (0,), rk + (1,)], writes=[("qr", sl, 0)])
                p.op("dve", lambda e: e.tensor_tensor(out=qr3[:, :, 8:16], in0=T[2], in1=T[3], op=ALU.add),
                     reads=[rk + (2,), rk + (3,)], writes=[("qr", sl, 1)])
                p.op("act", lambda e: e.copy(out=qr3[:, :, 16:64], in_=qn3[:, :, 16:64]), reads=[kqn], writes=[("qr", sl, 2)])
                tb = rot_T.next()
                for j in range(4):
                    p.op("pe", lambda e, j=j: e.transpose(out=psb[tb][:, j, :], in_=qr_s[:, j * 128:(j + 1) * 128], identity=identb),
                         reads=[("qr", sl, 0), ("qr", sl, 1), ("qr", sl, 2), "cb"], writes=[("ps", 6 + tb)])
                p.op("act", lambda e: e.copy(out=qkT[:, :, tt * 128:(tt + 1) * 128], in_=psb[tb][:, 0:4, :]),
                     reads=[("ps", 6 + tb)], writes=[("qkT", tt)])

            qk_mm(0)
            qk_mm(1)
            qk_mm(2)
            qk_Xa(0)
            qk_Xb(0)
            qk_Xa(1)
            qk_Xb(1)
            for tt in range(16):
                if tt + 3 < 16:
                    qk_mm(tt + 3)
                if tt + 2 < 16:
                    qk_Xa(tt + 2)
                qk_Y(tt)
                if tt + 2 < 16:
                    qk_Xb(tt + 2)
            p.retire(["qn", "qr", "rt", "sqj"])
            v_phase((2,))
            p.retire(["wqk", "wv", "qn", "qr"])
            allq = [("qkT", tt) for tt in range(16)]
            items = []
            for hh in range(4):
                h = 4 * hf + hh
                pair, odd = hh // 2, hh % 2
                base = 64 * odd
                qT = qkT[base:base + 64, pair, :]
                kT = qkT[base:base + 64, 2 + pair, :]
                va = lambda b, tile, pair=pair, odd=odd: Vaug[b][:, tile, pair, odd * 64:odd * 64 + 128]
                for g in range(8):
                    tiles = []
                    for j in (2 * g, 2 * g + 1):
                        nq = 256 if j < 15 else 128
                        pvs = []
                        for qb in range(nq // 128):
                            col0 = (j + qb) * 128
                            bank = col0 // 512
                            first = (j == 0 and qb == 0) or (qb == 1 and (j + 1) % 4 == 0)
                            pvs.append((bank, psf[bank][:, col0 % 512:col0 % 512 + 128], va(0, j), (qb * 128, (qb + 1) * 128), first, ("Va", 0, j)))
                        tiles.append((kT[:, j * 128:(j + 1) * 128], qT[:, j * 128:j * 128 + nq], nq,
                                      [("qkT", j)] + ([("qkT", j + 1)] if j < 15 else []), pvs))
                    items.append(("g", tiles))
                for rho in range(4):
                    for g in range(2):
                        tiles = []
                        for j in (2 * g, 2 * g + 1):
                            nq = 256 if j < 3 else 128
                            s0 = 512 * j + rho
                            pvs = []
                            for qb in range(nq // 128):
                                bank = j + qb
                                pvs.append((bank, psf[bank][:, rho:rho + 509:4], va(1, 4 * rho + j), (qb * 128, (qb + 1) * 128), False, ("Va", 1, 4 * rho + j)))
                            tiles.append((kT[:, s0:s0 + 509:4], qT[:, s0:s0 + 4 * (nq - 1) + 1:4], nq,
                                          [("qkT", 4 * j + i) for i in range(4 * (nq // 128))], pvs))
                        items.append(("g", tiles))
                for g in range(4):
                    tiles = []
                    for r in range(4 * g, 4 * g + 4):
                        pvs = [(bank, psf[bank][:, r:r + 497:16], va(2, r), (bank * 32, (bank + 1) * 32), False, ("Va", 2, r)) for bank in range(4)]
                        tiles.append((kT[:, r:r + 2033:16], qT[:, r:r + 2033:16], 128, allq, pvs))
                    items.append(("g", tiles))
                items.append(("fin", h, odd))

            def stage_S(tiles, state):
                sbk = rot_S.next()
                k = pti[0] % 6
                pti[0] += 1
                state["k"] = k
                pS = psf[sbk]
                col = 0
                offs = []
                for (kap, qap, nq, rkeys, pvs) in tiles:
                    p.op("pe", lambda e, kap=kap, qap=qap, col=col, nq=nq: e.matmul(pS[:, col:col + nq], lhsT=kap, rhs=qap, start=True, stop=True),
                         reads=rkeys, writes=PSK(sbk))
                    offs.append(col)
                    col += nq
                tot = col
                state["offs"] = offs
                p.op("act", lambda e: e.activation(out=pt[k][:, 0:tot], in_=pS[:, 0:tot], func=AF.Exp, scale=0.125),
                     reads=PSK(sbk), writes=[("pt", k)])
                nqs = [t[2] for t in tiles]
                if tot == 512 and len(set(nqs)) == 1:
                    n, w = len(nqs), nqs[0]
                    mv = cb[:, MSK:MSK + w].unsqueeze(1).to_broadcast([128, n, w])
                    pv_ = pt[k][:, :].rearrange("p (n w) -> p n w", w=w)
                    p.op("dve", lambda e: e.tensor_tensor(out=pv_, in0=pv_, in1=mv, op=ALU.mult), reads=[("pt", k), "cb"], writes=[("pt", k)])
                else:
                    for off, w in zip(offs, nqs):
                        p.op("dve", lambda e, off=off, w=w: e.tensor_tensor(out=pt[k][:, off:off + w], in0=pt[k][:, off:off + w],
                                                                          in1=cb[:, MSK:MSK + w], op=ALU.mult),
                             reads=[("pt", k), "cb"], writes=[("pt", k)])

            def stage_PV(tiles, state):
                k = state["k"]
                for (kap, qap, nq, rkeys, pvs), off in zip(tiles, state["offs"]):
                    for (bank, oap, lhs, (a, b_), first, vkey) in pvs:
                        p.op("pe", lambda e, oap=oap, lhs=lhs, a=a, b_=b_, off=off, first=first: e.matmul(
                            oap, lhsT=lhs, rhs=pt[k][:, off + a:off + b_], start=first, stop=False, skip_group_check=True),
                            reads=[vkey, ("pt", k), ("Va", vkey[1], "ones")], writes=PSK(bank))

            def stage_fin(h, odd):
                for bank in range(4):
                    if odd == 0:
                        nsl, dsl = slice(0, 64), slice(64, 128)
                    else:
                        nsl, dsl = slice(64, 128), slice(0, 64)
                    p.op("act", lambda e, bank=bank, nsl=nsl, dsl=dsl: e.activation(out=rden[nsl, :], in_=psf[bank][dsl, :], func=AF.Ln),
                         reads=PSK(bank), writes=["rden"])
                    p.op("act", lambda e, nsl=nsl: e.activation(out=rden[nsl, :], in_=rden[nsl, :], func=AF.Exp, scale=-1.0),
                         reads=["rden"], writes=["rden"])
                    p.op("dve", lambda e, bank=bank, nsl=nsl, h=h: e.tensor_tensor(
                        out=mixedT[nsl, h // 2, bank * 512:(bank + 1) * 512], in0=psf[bank][nsl, :], in1=rden[nsl, :], op=ALU.mult),
                        reads=PSK(bank) + ["rden"], writes=[("mixedT", h, bank)])

            L = 3
            states = [dict() for _ in items]
            for idx in range(len(items) + L):
                if idx < len(items) and items[idx][0] == "g":
                    stage_S(items[idx][1], states[idx])
                if idx >= L:
                    it = items[idx - L]
                    if it[0] == "g":
                        stage_PV(it[1], states[idx - L])
                    else:
                        stage_fin(it[1], it[2])

        hbi = [0]
        ami = [0]
        rot_hb = Rot([5, 7])
        hgc = [0]

        def hgrn():
            p.op("pool", lambda e: e.dma_start(out=wi[:, :, :], in_=win_v[:, :, 2560:3072]), writes=[("wi", 0)], dma=True)
            for tt in range(16):
                bk = rot_full.next()
                for kc in range(8):
                    p.op("pe", lambda e, kc=kc, bk=bk, tt=tt: e.matmul(psf[bk][:, :], lhsT=hTm[:, kc, tt * 128:(tt + 1) * 128], rhs=wi[:, kc, :],
                                                                     start=(kc == 0), stop=(kc == 7)),
                         reads=[("hTm", tt), ("wi", 0)], writes=PSK(bk))
                if tt % 2 == 0:
                    p.op("act", lambda e, bk=bk, tt=tt: e.copy(out=vtm[:, tt, :], in_=psf[bk][:, :]), reads=PSK(bk), writes=[("vtm", tt)])
                else:
                    p.op("dve", lambda e, bk=bk, tt=tt: e.tensor_copy(out=vtm[:, tt, :], in_=psf[bk][:, :]), reads=PSK(bk), writes=[("vtm", tt)])
            p.op("dve", lambda e: e.memset(hscm[:, :], 1.0), writes=["hscm"])
            p.op("dve", lambda e: e.memset(hscm[:, 0:512:64], 0.0), writes=["hscm"])
            blocks = [(h, cbk) for h in range(4) for cbk in range(4)]
            sts = [dict() for _ in blocks]
            for bi_, st_ in enumerate(sts):
                st_["sl"] = hbi[0] % 2
                st_["sg"] = bi_ % 3
                st_["bo"] = 3 + (bi_ % 2)
                hbi[0] += 1
            def interleave(gens):
                live = [[g, p.eng_free["pe"] * 0.0] for (g, w) in gens]
                while live:
                    it = min(live, key=lambda x: x[1])
                    try:
                        next(it[0])
                        it[1] = p.last_end
                    except StopIteration:
                        live.remove(it)

            nb_ = len(blocks)
            interleave([(hg_front(0, 0, sts[0]), 1)])
            p.retire(["wi"])
            p.op("pool", lambda e: e.dma_start(out=woA[:, :, :], in_=wout_v[:, :, 0:512]), writes=[("woA", 0)], dma=True)
            for i in range(nb_ + 1):
                gens = []
                if i + 1 < nb_:
                    gens.append((hg_front(blocks[i + 1][0], blocks[i + 1][1], sts[i + 1]), 2))
                if i < nb_:
                    gens.append((hg_back(blocks[i][0], blocks[i][1], sts[i]), 3))
                if i >= 1:
                    gens.append((hg_out(blocks[i - 1][0], blocks[i - 1][1], sts[i - 1]), 1))
                interleave(gens)

        def hg_front(h, cbk, stt):
            wsl = h % 2
            if cbk == 0:
                for i, c0 in enumerate((1536, 2048, 3072)):
                    p.op("pool", lambda e, i=i, c0=c0: e.dma_start(out=wqfg2[wsl][:, :, i, :], in_=win_v[:, :, c0 + h * 128:c0 + (h + 1) * 128]),
                         writes=[("wqfg", wsl, i)], dma=True)
                    yield
            lb_h = lbt[:, h:h + 1]
            oml_h = lbt[:, 4 + h:5 + h]
            sl = stt["sl"]
            sgs = stt["sg"]
            cols = slice(cbk * 512, (cbk + 1) * 512)
            hkeys = [("hTm", cbk * 4 + i) for i in range(4)]
            bq, bf_, bg = 0, 1, 2
            for i, bk in ((1, bf_), (2, bg), (0, bq)):
                for kc in range(8):
                    p.op("pe", lambda e, kc=kc, bk=bk, i=i: e.matmul(psf[bk][:, :], lhsT=wqfg2[wsl][:, kc, i, :], rhs=hTm[:, kc, cols],
                                                                   start=(kc == 0), stop=(kc == 7)),
                         reads=hkeys + [("wqfg", wsl, i)], writes=PSK(bk), dur=0.3)
                yield
            A, B, C = hA[sl], hB[sl], hC[sl]
            p.op("act", lambda e: e.activation(out=A[:, :], in_=psf[bf_][:, :], func=AF.Sigmoid), reads=PSK(bf_), writes=[("hA", sl)], aset="sig")
            yield
            p.op("act", lambda e: e.activation(out=B[:, :], in_=psf[bf_][:, :], func=AF.Sigmoid, scale=-1.0), reads=PSK(bf_), writes=[("hB", sl)], aset="sig")
            yield
            p.op("act", lambda e: e.activation(out=silg[sgs][:, :], in_=psf[bg][:, :], func=AF.Sigmoid), reads=PSK(bg), writes=[("silg", sgs)], aset="sig")
            yield
            p.op("dve", lambda e: e.tensor_tensor(out=silg[sgs][:, :], in0=psf[bg][:, :], in1=silg[sgs][:, :], op=ALU.mult),
                 reads=PSK(bg) + [("silg", sgs)], writes=[("silg", sgs)])
            yield
            p.op("act", lambda e: e.activation(out=A[:, :], in_=A[:, :], func=AF.Ln, scale=oml_h, bias=lb_h),
                 reads=[("hA", sl), "lbt"], writes=[("hA", sl)], aset="le")
            yield
            p.op("dve", lambda e: e.tensor_tensor_scan(out=C[:, :], data0=hscm[:, :], data1=A[:, :], initial=0.0, op0=ALU.mult, op1=ALU.add),
                 reads=[("hA", sl), "hscm"], writes=[("hC", sl)], dur=1.25)
            yield
            p.op("act", lambda e: e.activation(out=A[:, :], in_=C[:, :], func=AF.Exp), reads=[("hC", sl)], writes=[("hA", sl)], aset="le")
            yield
            p.op("act", lambda e: e.activation(out=C[:, :], in_=C[:, :], func=AF.Exp, scale=-1.0), reads=[("hC", sl)], writes=[("hC", sl)], aset="le")
            yield
            p.op("dve", lambda e: e.tensor_tensor(out=qd[sl][:, :], in0=psf[bq][:, :], in1=A[:, :], op=ALU.mult),
                 reads=PSK(bq) + [("hA", sl)], writes=[("qd", sl)])
            yield
            p.op("dve", lambda e: e.scalar_tensor_tensor(out=kd[sl][:, :], in0=B[:, :], scalar=oml_h, in1=C[:, :], op0=ALU.mult, op1=ALU.mult),
                 reads=[("hB", sl), ("hC", sl), "lbt"], writes=[("kd", sl)])
            yield
            dec = A[:, 63:512:64]
            p.op("dve", lambda e: e.tensor_tensor(out=kk[sl].rearrange("p (c t) -> p c t", t=64),
                                                  in0=kd[sl].rearrange("p (c t) -> p c t", t=64),
                                                  in1=dec.unsqueeze(2).to_broadcast([128, 8, 64]), op=ALU.mult),
                 reads=[("kd", sl), ("hA", sl)], writes=[("kk", 0)])
            yield
            tb = 0
            for i in range(4):
                p.op("pe", lambda e, i=i: e.transpose(out=psb[tb][:, i, :], in_=kk[sl][:, i * 128:(i + 1) * 128], identity=identb),
                     reads=[("kk", 0), "cb"], writes=[("ps", 6 + tb)])
            p.op("act", lambda e: e.copy(out=kktm[sl][:, :, :], in_=psb[tb][:, 0:4, :]), reads=[("ps", 6 + tb)], writes=[("kktm", sl)])
            yield

        def hg_back(h, cbk, stt):
            sl = stt["sl"]
            A = hA[sl]
            cols = slice(cbk * 512, (cbk + 1) * 512)
            if cbk == 0:
                p.op("dve", lambda e: e.memset(Sf[:, 0, :], 0.0), writes=[("Sf", 0)])
                yield
                p.op("dve", lambda e: e.memset(Sbf[sl][:, 0, :], 0.0), writes=[("Sbf", sl, 0)])
                yield
                hgc[0] = 0
            bo = stt["bo"]
            pend = None
            for i in range(4):
                tt = cbk * 4 + i
                ba = rot_hb.next()
                while ba == bo:
                    ba = rot_hb.next()
                am = ami[0] % 4
                ami[0] += 1
                p.op("pe", lambda e, i=i, ba=ba: e.matmul(psf[ba][:, 0:128], lhsT=kd[sl][:, i * 128:(i + 1) * 128], rhs=qd[sl][:, i * 128:(i + 1) * 128],
                                                         start=True, stop=True),
                     reads=[("kd", sl), ("qd", sl)], writes=PSK(ba))
                p.op("dve", lambda e, ba=ba, am=am: e.tensor_tensor(out=attm[am][:, :], in0=psf[ba][:, 0:128], in1=hmask, op=ALU.mult),
                     reads=PSK(ba) + ["cb"], writes=[("attm", am)], dur=0.3)
                yield
                for cc in range(2):
                    c = i * 2 + cc
                    bc = rot_hb.next()
                    while bc == bo:
                        bc = rot_hb.next()
                    p.op("pe", lambda e, i=i, cc=cc, bc=bc, tt=tt: e.matmul(
                        psf[bc][:, 0:128], lhsT=kktm[sl][cc * 64:(cc + 1) * 64, i, :], rhs=vtm[cc * 64:(cc + 1) * 64, tt, h * 128:(h + 1) * 128],
                        start=True, stop=True),
                        reads=[("kktm", sl), ("vtm", tt)], writes=PSK(bc))
                    so, sn = hgc[0] % 2, (hgc[0] + 1) % 2
                    hgc[0] += 1
                    p.op("dve", lambda e, so=so, sn=sn, c=c, bc=bc: e.scalar_tensor_tensor(
                        out=Sf[:, sn, :], in0=Sf[:, so, :], scalar=A[:, c * 64 + 63:c * 64 + 64], in1=psf[bc][:, 0:128],
                        op0=ALU.mult, op1=ALU.add),
                        reads=[("Sf", so), ("hA", sl)] + PSK(bc), writes=[("Sf", sn)], dur=0.35)
                    yield
                    if c < 7:
                        dkey = ("Sbf", sl, c + 1)
                        dap = Sbf[sl][:, c + 1, :]
                    else:
                        dkey = ("Sbf", 1 - sl, 0)
                        dap = Sbf[1 - sl][:, 0, :]
                    p.op("pool", lambda e, sn=sn, dap=dap: e.tensor_copy(out=dap, in_=Sf[:, sn, :]), reads=[("Sf", sn)], writes=[dkey], dur=0.8)
                    yield

                def emit_po(i=i, tt=tt, am=am):
                    oap = psf[bo][:, i * 128:(i + 1) * 128]
                    okey = PSK(bo)
                    p.op("pe", lambda e: e.matmul(oap, lhsT=vtm[:, tt, h * 128:(h + 1) * 128], rhs=attm[am][:, :], start=True, stop=False,
                                                  skip_group_check=True),
                         reads=[("vtm", tt), ("attm", am)], writes=okey)
                    for cc in range(2):
                        c = i * 2 + cc
                        p.op("pe", lambda e, cc=cc, c=c: e.matmul(oap[:, cc * 64:(cc + 1) * 64], lhsT=Sbf[sl][:, c, :],
                                                                 rhs=qd[sl][:, i * 128 + cc * 64:i * 128 + (cc + 1) * 64],
                                                                 start=False, stop=(cc == 1), skip_group_check=True),
                             reads=[("Sbf", sl, c), ("qd", sl)], writes=okey)
                    yield
                if pend is not None:
                    yield from pend()
                pend = emit_po
            yield from pend()

        def hg_out(h, cbk, stt):
            sl = stt["sl"]
            sgs = stt["sg"]
            bo = stt["bo"]
            cols = slice(cbk * 512, (cbk + 1) * 512)
            p.op("act", lambda e: e.activation(out=hsq[:, :], in_=psf[bo][:, :], func=AF.Square), reads=PSK(bo), writes=["hsq"])
            yield
            bs = rot_hb.next()
            while bs == bo:
                bs = rot_hb.next()
            p.op("act", lambda e: e.copy(out=hhi[:, :], in_=hsq[:, :]), reads=["hsq"], writes=["hhi"])
            yield
            p.op("dve", lambda e: e.tensor_tensor(out=hlo[:, :], in0=hsq[:, :], in1=hhi[:, :], op=ALU.subtract),
                 reads=["hsq", "hhi"], writes=["hlo"])
            yield
            p.op("pe", lambda e: e.matmul(psf[bs][:, :], lhsT=cb[:, ONEB:ONEB + 128], rhs=hhi[:, :], start=True, stop=False),
                 reads=["hhi", "cb"], writes=PSK(bs))
            p.op("pe", lambda e: e.matmul(psf[bs][:, :], lhsT=cb[:, ONEB:ONEB + 128], rhs=hlo[:, :], start=False, stop=True),
                 reads=["hlo", "cb"], writes=PSK(bs))
            p.op("act", lambda e: e.activation(out=hrstd[:, :], in_=psf[bs][:, :], func=AF.Ln, bias=EPS, scale=1.0),
                 reads=PSK(bs), writes=["hrstd"], aset="le")
            yield
            p.op("act", lambda e: e.activation(out=hrstd[:, :], in_=hrstd[:, :], func=AF.Exp, scale=-0.5), reads=["hrstd"], writes=["hrstd"], aset="le")
            yield
            p.op("dve", lambda e: e.scalar_tensor_tensor(out=hsq[:, :], in0=psf[bo][:, :], scalar=cf[:, GOUT:GOUT + 1], in1=hrstd[:, :],
                                                         op0=ALU.mult, op1=ALU.mult),
                 reads=PSK(bo) + ["hrstd", "cf", "hhi", "hlo"], writes=["hsq"])
            yield
            p.op("dve", lambda e: e.tensor_tensor(out=mixedT[:, 4 + h, cols], in0=hsq[:, :], in1=silg[sgs][:, :], op=ALU.mult),
                 reads=["hsq", ("silg", sgs)], writes=[("mixedT", 8 + h, cbk)])
            yield

        def wout(ffn_norm=False):
            p.op("pool", lambda e: e.dma_start(out=woB[:, :, :], in_=wout_v[:, :, 512:1024]), writes=[("woB", 0)], dma=True)
            mkeys = [("mixedT", h, b) for h in range(8) for b in range(4)] + [("mixedT", 8 + h, b) for h in range(4) for b in range(4)]
            due = {}
            for dh, (wsb, wkey) in enumerate(((woA, ("woA", 0)), (woB, ("woB", 0)))):
                for tt in range(16):
                    g_ = dh * 16 + tt
                    for fb in due.pop(g_, []):
                        fb()
                    bk = rot_full.next()
                    for kc in range(8):
                        p.op("pe", lambda e, kc=kc, bk=bk, tt=tt, wsb=wsb: e.matmul(
                            psf[bk][:, :], lhsT=mixedT[:, kc, tt * 128:(tt + 1) * 128], rhs=wsb[:, kc, :],
                            start=(kc == 0), stop=(kc == 7)),
                            reads=mkeys + [wkey], writes=PSK(bk))
                    p.op("dve", lambda e, bk=bk, tt=tt, dh=dh: e.tensor_tensor(
                        out=resid[:, tt, dh * 512:(dh + 1) * 512], in0=psf[bk][:, :], in1=resid[:, tt, dh * 512:(dh + 1) * 512], op=ALU.add),
                        reads=PSK(bk) + [("res", tt)], writes=[("res", tt)])
                    if ffn_norm and dh == 1 and tt < 8:
                        due.setdefault(g_ + 2, []).append(norm_tile(tt, hTf[:, :, tt * 128:(tt + 1) * 128], ("hTf", tt), 2, xn_f, defer=True))
            for k_ in sorted(due):
                for fb in due[k_]:
                    fb()

        def load_x(seq, g0, g1):
            for g in range(g0, g1):
                p.op("sp", lambda e, g=g, seq=seq: e.dma_start(out=resid[:, 2 * g:2 * g + 2, :], in_=x_v[:, seq * 16 + 2 * g:seq * 16 + 2 * g + 2, :]),
                     writes=[("res", 2 * g), ("res", 2 * g + 1)], dma=True)

        chain = (stage >= 3)
        for seq in range(nseq):
            first = (seq == 0) or not chain
            if first:
                load_x(seq, 0, 8)
            ffn(seq, 0, w1a, w3a, w2a, 0, store=(stage == 1), do_norm=first, norm_next=True)
            ffn(seq, 1, w1a, w3a, w2a, 0, store=(stage == 1), do_norm=False, mix_norm=(stage >= 2))
            p.retire(FFN_NAMES)
            if stage >= 2:
                if dbg and seq == 0:
                    p.op("sp", lambda e: e.dma_start(out=dbg_hT[:, :, :], in_=hTm[:, :, :]), reads=[("hTm", tt) for tt in range(16)], dma=True)
                p.retire(["xn"])
                for hf in range(2):
                    if os.environ.get("SKIP_ATT"):
                        continue
                    attention_half(hf)
                    p.retire(ATT_NAMES)
                if not os.environ.get("SKIP_HG"):
                    hgrn()
                p.retire(HG_NAMES + ["hTm"])
                if dbg and seq == 0:
                    p.op("sp", lambda e: e.dma_start(out=dbg_mixed[:, :, :], in_=mixedT[:, :, :]),
                         reads=[("mixedT", h, b) for h in range(12) for b in range(4)], dma=True)
                wout(ffn_norm=(stage >= 3))
                p.retire(["woA", "woB", "mixedT"])
                if stage == 2:
                    for tt in range(16):
                        p.op("sp", lambda e, tt=tt, seq=seq: e.dma_start(out=out_v[:, seq * 16 + tt, :], in_=resid[:, tt, :]),
                             reads=[("res", tt)], dma=True)
            if stage >= 3:
                ffn(seq, 0, w1b_, w3b_, w2b_, 2, store=True, do_norm=False, norm_next=True)
                more = chain and seq + 1 < nseq
                if more:
                    load_x(seq + 1, 0, 4)
                ffn(seq, 1, w1b_, w3b_, w2b_, 2, store=True, do_norm=False, next_seq_norm=more)
                if more:
                    load_x(seq + 1, 4, 8)
                else:
                    p.retire(FFN_NAMES)
            p.retire(["hTm"])
        p.emit(st)
        build.stats = {e: len([o for o in p.ops if o.eng == e]) for e in ENGS}
        build.maxcnt = p.max_cnt
    return nc


def make_consts(inputs):
    f32 = np.float32
    cf = np.zeros((128, NCF), f32)
    pidx = np.arange(128)
    for n, name in enumerate(("ffn1_norm", "mix_norm", "ffn2_norm")):
        g = np.asarray(inputs[name], f32).reshape(8, 128)
        cf[:, GAM + n * 8:GAM + n * 8 + 8] = g.T
    inv = (f32(500000.0) ** (-(np.arange(0, 16, 2, dtype=f32)) / f32(16))).astype(f32)
    pos = (np.arange(16)[None, :] * 128 + pidx[:, None]).astype(f32)
    ang = (pos[:, :, None] * inv[None, None, :]).astype(f32)
    cf[:, COS:COS + 128] = np.cos(ang).astype(f32).reshape(128, 128)
    cf[:, SIN:SIN + 128] = np.sin(ang).astype(f32).reshape(128, 128)
    qn = np.asarray(inputs["q_norm"], f32).reshape(64)
    kn = np.asarray(inputs["k_norm"], f32).reshape(64)
    cf[:, GQK:GQK + 512] = np.concatenate([np.tile(qn, 4), np.tile(kn, 4)])[None, :]
    lbl = np.asarray(inputs["hg_lb_logits"], f32).reshape(2, 4, 128)
    cf[:, LBL:LBL + 8] = lbl.transpose(2, 0, 1).reshape(128, 8)
    cf[:, GOUT] = np.asarray(inputs["hg_out_norm"], f32).reshape(128)
    cf[:, ONESF:ONESF + 128] = 1.0 / 128.0
    scm = np.ones(64, f32)
    scm[0] = 0.0
    cf[:, SCM:SCM + 64] = scm[None, :]
    cb = np.zeros((128, NCB), f32)
    cb[:, IDB:IDB + 128] = np.eye(128, dtype=f32)
    cb[:, ONEB:ONEB + 128] = 1.0 / 128.0
    s_ = pidx[:, None]
    t_ = np.arange(256)[None, :]
    cb[:, MSK:MSK + 256] = ((t_ - s_ >= 0) & (t_ - s_ <= 128)).astype(f32)
    t2 = np.arange(128)[None, :]
    cb[:, HMK:HMK + 128] = ((s_ // 64 == t2 // 64) & (s_ <= t2)).astype(f32)
    return cf, cb.astype(ml_dtypes.bfloat16)


_NC_CACHE = {}


def run(inputs, stage=3, dbg=False, cores=NCORES, trace=False):
    key = (stage, dbg)
    if key not in _NC_CACHE:
        _NC_CACHE[key] = build(stage, dbg)
    nc = _NC_CACHE[key]
    cf, cb = make_consts(inputs)
    x = np.ascontiguousarray(np.asarray(inputs["x"], np.float32)).reshape(NCORES, TOK, D)
    shared = {
        "cf": cf, "cb": cb,
        "w_in": np.ascontiguousarray(np.asarray(inputs["w_in"], np.float32)[0]),
        "w_out": np.ascontiguousarray(np.asarray(inputs["w_out"], np.float32)[0]),
    }
    for n in ("ffn1_w1", "ffn1_w3", "ffn1_w2", "ffn2_w1", "ffn2_w3", "ffn2_w2"):
        shared[n] = np.ascontiguousarray(np.asarray(inputs[n], np.float32)[0])
    in_maps = [dict(shared, x=x[i]) for i in range(cores)]
    res = run_bass_kernel_spmd(nc, in_maps, core_ids=list(range(cores)), **({"trace": True} if trace else {}))
    return res


def kernel(**inputs):
    res = run(inputs)
    out = np.stack([np.asarray(r["out"], np.float32) for r in res.results], axis=0)
    return out.reshape(16, S, D)
```

```python
import os
import numpy as np
import ml_dtypes
from contextlib import ExitStack
import concourse.bass as bass
import concourse.mybir as mybir
from concourse.bass_utils import run_bass_kernel_spmd

F32 = mybir.dt.float32
BF16 = mybir.dt.bfloat16
AF = mybir.ActivationFunctionType
ALU = mybir.AluOpType
AX = mybir.AxisListType
EPS = 1e-6

NCORES = 8
S = 2048
D = 1024
DFF = 2816
NFC = 22
TOK = 2 * S

GAM, COS, SIN, GQK, LBL, GOUT, ONESF, SCM = 0, 24, 152, 280, 792, 800, 801, 929
NCF = 996
IDB, MSK, HMK, ONEB = 0, 128, 384, 512
NCB = 640

ARENA_BYTES = 105472

ENGS = ("pe", "act", "dve", "pool", "sp")


class Op:
    __slots__ = ("eng", "fn", "dma", "deps", "flag", "cnt", "dsem", "dval", "idx", "tend")

    def __init__(self, eng, fn, dma):
        self.eng = eng
        self.fn = fn
        self.dma = dma
        self.deps = []
        self.flag = False
        self.cnt = 0
        self.dsem = None
        self.dval = 0


class Prog:
    def __init__(self, nc, ndma_sems=16):
        self.nc = nc
        self.ops = []
        self.last_w = {}
        self.readers = {}
        self.ndma = ndma_sems
        self.pending = {}
        self.pending_dma = []
        self.eng_free = {e: 0.0 for e in ENGS}
        self.last_end = 0.0

    def retire(self, names):
        names = set(names)
        dead = [k for k in list(self.last_w.keys()) + list(self.readers.keys())
                if (k[0] if isinstance(k, tuple) else k) in names]
        for k in set(dead):
            ops = []
            w = self.last_w.pop(k, None)
            if w is not None:
                ops.append(w)
            ops.extend(self.readers.pop(k, ()))
            for o in ops:
                if o.dma:
                    if o not in self.pending_dma:
                        self.pending_dma.append(o)
                else:
                    cur = self.pending.get(o.eng)
                    if cur is None or cur.idx < o.idx:
                        self.pending[o.eng] = o

    def op(self, eng, fn, reads=(), writes=(), dma=False, dur=None, aset=None):
        o = Op(eng, fn, dma)
        o.idx = len(self.ops)
        if dur is None:
            dur = 2.0 if dma else (0.12 if eng == "pe" else (0.63 if eng == "act" else 0.7))
        if aset is not None:
            if getattr(self, "cur_aset", None) not in (None, aset):
                dur += 1.3
            self.cur_aset = aset
        deps = set()
        for k in reads:
            w = self.last_w.get(k)
            if w is not None:
                deps.add(w)
        for k in writes:
            if k not in self.last_w and k not in self.readers:
                deps.update(self.pending.values())
                deps.update(self.pending_dma)
            w = self.last_w.get(k)
            if w is not None:
                deps.add(w)
            for r in self.readers.get(k, ()):
                deps.add(r)
        for k in writes:
            self.last_w[k] = o
            self.readers[k] = []
        for k in reads:
            self.readers.setdefault(k, []).append(o)
        deps.discard(o)
        for d in deps:
            if (not d.dma) and (not o.dma) and d.eng == o.eng and o.eng == "pe":
                continue
            o.deps.append(d)
        t0 = self.eng_free[eng] if not dma else 0.0
        for d in deps:
            if d.tend + 0.1 > t0:
                t0 = d.tend + 0.1
        o.tend = t0 + dur
        if not dma:
            self.eng_free[eng] = o.tend
        self.last_end = o.tend
        self.ops.append(o)
        return o

    def emit(self, stack):
        nc = self.nc
        per = {e: [o for o in self.ops if o.eng == e] for e in ENGS}
        for o in self.ops:
            for d in o.deps:
                if not d.dma:
                    d.flag = True
        esem = {e: stack.enter_context(nc.semaphore("s_" + e)) for e in ENGS if e != "sp"}
        for e in ENGS:
            c = 0
            for o in per[e]:
                if not o.dma and o.flag:
                    c += 1
                    o.cnt = c
        extra_wait = {}
        for e in ENGS:
            dl = [o for o in per[e] if o.dma]
            if not dl:
                continue
            sems = [stack.enter_context(nc.semaphore("d_%s%d" % (e, i))) for i in range(self.ndma)]
            for j, o in enumerate(dl):
                o.dsem = sems[j % self.ndma]
                o.dval = 16 * (j // self.ndma + 1)
                if j >= self.ndma:
                    extra_wait[o] = dl[j - self.ndma]
        self.max_cnt = {e: max([o.cnt for o in per[e]] + [0]) for e in ENGS}
        block = stack.enter_context(nc.Block())
        handles = {"pe": block.tensor, "act": block.scalar, "dve": block.vector,
                   "pool": block.gpsimd, "sp": block.sync}

        def run(e, eng):
            waited = {}
            for o in per[e]:
                deps = list(o.deps)
                if o in extra_wait:
                    deps.append(extra_wait[o])
                need = {}
                for d in deps:
                    if d.dma:
                        s, v = d.dsem, d.dval
                    else:
                        s, v = esem[d.eng], d.cnt
                    key = id(s)
                    if waited.get(key, 0) >= v:
                        continue
                    if key not in need or need[key][1] < v:
                        need[key] = (s, v)
                for key, (s, v) in need.items():
                    eng.wait_ge(s, v)
                    waited[key] = v
                ins = o.fn(eng)
                if o.dma:
                    ins.then_inc(o.dsem, 16)
                elif o.flag:
                    ins.then_inc(esem[e], 1)
            last = {}
            for o in per[e]:
                if o.dma:
                    last[id(o.dsem)] = (o.dsem, o.dval)
            for key, (s, v) in last.items():
                if waited.get(key, 0) < v:
                    eng.wait_ge(s, v)

        for e in ENGS:
            if per[e]:
                handles[e]((lambda ee: (lambda eng: run(ee, eng)))(e))


class Rot:
    def __init__(self, items):
        self.items = list(items)
        self.i = 0

    def next(self):
        it = self.items[self.i % len(self.items)]
        self.i += 1
        return it


def build(stage=3, dbg=False, nseq=2):
    nc = bass.Bass("TRN2", target_bir_lowering=False)

    def din(name, shape, dt=F32):
        return nc.dram_tensor(name, shape, dt, kind="ExternalInput").ap()

    x_d = din("x", [TOK, D])
    out_d = nc.dram_tensor("out", [TOK, D], F32, kind="ExternalOutput").ap()
    w1a = din("ffn1_w1", [D, DFF]); w3a = din("ffn1_w3", [D, DFF]); w2a = din("ffn1_w2", [DFF, D])
    w1b_ = din("ffn2_w1", [D, DFF]); w3b_ = din("ffn2_w3", [D, DFF]); w2b_ = din("ffn2_w2", [DFF, D])
    win_d = din("w_in", [D, 3584]); wout_d = din("w_out", [D, D])
    cf_d = din("cf", [128, NCF]); cb_d = din("cb", [128, NCB], BF16)
    if dbg:
        dbg_mixed = nc.dram_tensor("dbg_mixed", [128, 8, S], BF16, kind="ExternalOutput").ap()
        dbg_hT = nc.dram_tensor("dbg_hT", [128, 8, S], BF16, kind="ExternalOutput").ap()

    x_v = x_d.rearrange("(t p) d -> p t d", p=128)
    out_v = out_d.rearrange("(t p) d -> p t d", p=128)
    kv = lambda w: w.rearrange("(k p) n -> p k n", p=128)
    win_v = kv(win_d)
    wout_v = kv(wout_d)

    with ExitStack() as st:
        sb = lambda name, shape, dt: st.enter_context(nc.sbuf_tensor(name, shape, dt))
        resid = sb("resid", [128, 16, D], F32)
        hTb = sb("hTb", [128, 16384], BF16)
        cf = sb("cf_sb", [128, NCF], F32)
        cb = sb("cb_sb", [128, NCB], BF16)
        ms = sb("ms", [128, 96], F32)
        rs = sb("rs", [128, 96], F32)
        lbt = sb("lbt", [128, 8], F32)
        ss8 = sb("ss8", [128, 4, 8], F32)
        rs8 = sb("rs8", [128, 4, 8], F32)
        Sf = sb("Sf", [128, 2, 128], F32)
        arena = sb("arena", [128, ARENA_BYTES // 2], BF16)
        psf = [st.enter_context(nc.psum_tensor("ps%d" % i, [128, 512], F32)) for i in range(8)]
        psb = [psf[6 + i][:, :].bitcast(BF16).rearrange("p (k t) -> p k t", t=128) for i in range(2)]

        def carve(off, shape, dt):
            n = int(np.prod(shape[1:]))
            nb = n * (4 if dt == F32 else 2)
            assert off % 4 == 0 and off + nb <= ARENA_BYTES, (off, shape)
            a = arena[:, off // 2:(off + nb) // 2]
            if dt == F32:
                a = a.bitcast(F32)
            if len(shape) == 3:
                a = a.rearrange("p (a b) -> p a b", b=shape[2])
            elif len(shape) == 4:
                a = a.rearrange("p (a b c) -> p a b c", b=shape[2], c=shape[3])
            elif len(shape) == 5:
                a = a.rearrange("p (a b c d) -> p a b c d", b=shape[2], c=shape[3], d=shape[4])
            return a

        hTm = hTb.rearrange("p (k t) -> p k t", t=2048)
        hTf = hTb[:, 0:8192].rearrange("p (k t) -> p k t", t=1024)
        w13 = [hTb[:, 8192 + i * 2048: 8192 + (i + 1) * 2048].rearrange("p (k n) -> p k n", n=256) for i in range(4)]
        w1s, w3s = w13[0:2], w13[2:4]

        gT = carve(0, [128, NFC, 1024], BF16)
        w2sb = carve(45056, [128, NFC, 1024], BF16)
        sg = [carve(90112 + i * 2048, [128, 512], F32) for i in range(2)]
        xn_f = [carve(94208 + i * 2048, [128, 1024], BF16) for i in range(4)]
        mixedT = carve(0, [128, 8, 2048], BF16)
        A0 = 32768
        xn_m = [carve(A0 + i * 2048, [128, 1024], BF16) for i in range(2)]
        o = A0
        wqk = carve(o, [128, 8, 512], BF16); o += 8192
        wv = carve(o, [128, 8, 256], BF16); o += 4096
        qkT = carve(o, [128, 4, 2048], BF16); o += 16384
        Vaug = []
        for b in range(3):
            Vaug.append(carve(o, [128, 16, 2, 192], BF16)); o += 12288
        rden = carve(o, [128, 512], F32); o += 2048
        o += 2048
        pt = [carve(A0 + i * 1024, [128, 512], BF16) for i in range(6)]
        VA2 = A0 + 8192 + 4096 + 16384 + 2 * 12288
        qn2 = [carve(VA2 + i * 2048, [128, 512], F32) for i in range(2)]
        qr2 = [carve(VA2 + 4096 + i * 1024, [128, 512], BF16) for i in range(2)]
        rt2 = [carve(VA2 + 6144 + i * 1024, [128, 4, 64], F32) for i in range(2)]
        qn_a = carve(o, [128, 512], F32); o += 2048
        sq_a = qn_a
        qr_a = carve(o, [128, 512], BF16); o += 1024
        assert o <= ARENA_BYTES, o
        o = A0
        wi = carve(o, [128, 8, 512], BF16); o += 8192
        wqfg2 = []
        for i in range(2):
            wqfg2.append(carve(o, [128, 8, 3, 128], BF16)); o += 6144
        vtm = carve(o, [128, 16, 512], BF16); o += 16384
        hA, hB, hC = [], [], []
        for i in range(2):
            hA.append(carve(o, [128, 512], F32)); o += 2048
            hB.append(carve(o, [128, 512], F32)); o += 2048
            hC.append(carve(o, [128, 512], F32)); o += 2048
        qd, kd, kk, kktm, Sbf, silg = [], [], [], [], [], []
        for i in range(2):
            qd.append(carve(o, [128, 512], BF16)); o += 1024
            kd.append(carve(o, [128, 512], BF16)); o += 1024
            if i == 0:
                kk.append(carve(o, [128, 512], BF16)); o += 1024
            else:
                kk.append(kk[0])
            kktm.append(carve(o, [128, 4, 128], BF16)); o += 1024
            Sbf.append(carve(o, [128, 8, 128], BF16)); o += 2048
            silg.append(carve(o, [128, 512], BF16)); o += 1024
        silg.append(carve(o, [128, 512], BF16)); o += 1024
        hsq = carve(o, [128, 512], F32); o += 2048
        hrstd = carve(o, [128, 512], F32); o += 2048
        attm = []
        for i in range(4):
            attm.append(carve(o, [128, 128], BF16)); o += 256
        hscm = carve(o, [128, 512], F32); o += 2048
        hhi = carve(o, [128, 512], BF16); o += 1024
        hlo = carve(o, [128, 512], BF16); o += 1024
        assert o <= ARENA_BYTES, o
        woA = carve(A0, [128, 8, 512], BF16)
        woB = carve(A0 + 8192, [128, 8, 512], BF16)

        identb = cb[:, IDB:IDB + 128]
        amask = cb[:, MSK:MSK + 256]
        hmask = cb[:, HMK:HMK + 128]

        p = Prog(nc)
        PSK = lambda b, h=None: [("ps", b)]

        p.op("sp", lambda e: e.dma_start(out=cf[:, :], in_=cf_d[:, :]), writes=["cf"], dma=True)
        p.op("sp", lambda e: e.dma_start(out=cb[:, :], in_=cb_d[:, :]), writes=["cb"], dma=True)
        p.op("dve", lambda e: e.memset(ms[:, :], 0.0), writes=["ms_all"])
        p.op("dve", lambda e: e.tensor_tensor(out=lbt[:, 0:4], in0=cf[:, LBL:LBL + 4], in1=cf[:, LBL + 4:LBL + 8], op=ALU.subtract),
             reads=["cf"], writes=["lbt0"])
        p.op("dve", lambda e: e.tensor_tensor(out=lbt[:, 4:8], in0=cf[:, LBL + 4:LBL + 8], in1=cf[:, LBL:LBL + 4], op=ALU.subtract),
             reads=["cf"], writes=["lbt1"])
        p.op("act", lambda e: e.activation(out=lbt[:, :], in_=lbt[:, :], func=AF.Sigmoid), reads=["lbt0", "lbt1"], writes=["lbt"])

        nidx = [0]
        rot_T = Rot([0, 1])

        def norm_tile(tt, dst, dst_key, gidx, xn, defer=False):
            i = nidx[0]
            nidx[0] += 1
            slot = i % len(xn)
            xs = xn[slot]
            p.op("act", lambda e: e.activation(out=xs[:, :], in_=resid[:, tt, :], func=AF.Square, scale=1.0 / 32.0,
                                               accum_out=ms[:, i:i + 1]),
                 reads=[("res", tt), "ms_all"], writes=[("xn", slot), ("ms", i)])
            p.op("act", lambda e: e.activation(out=rs[:, i:i + 1], in_=ms[:, i:i + 1], func=AF.Sqrt, bias=EPS, scale=1.0),
                 reads=[("ms", i)], writes=[("rs", i)])
            p.op("dve", lambda e: e.reciprocal(out=rs[:, i:i + 1], in_=rs[:, i:i + 1]), reads=[("rs", i)], writes=[("rs", i)])
            p.op("dve", lambda e: e.tensor_scalar(out=xs[:, :], in0=resid[:, tt, :], scalar1=rs[:, i:i + 1], scalar2=None, op0=ALU.mult),
                 reads=[("res", tt), ("rs", i)], writes=[("xn", slot)])
            def part_b():
                tb = rot_T.next()
                for kc in range(8):
                    p.op("pe", lambda e, kc=kc: e.transpose(out=psb[tb][:, kc, :], in_=xs[:, kc * 128:(kc + 1) * 128], identity=identb),
                         reads=[("xn", slot), "cb"], writes=[("ps", 6 + tb)])
                gb = cf[:, GAM + gidx * 8:GAM + gidx * 8 + 8].unsqueeze(2).to_broadcast([128, 8, 128])
                p.op("dve", lambda e: e.tensor_tensor(out=dst, in0=psb[tb][:, :, :], in1=gb, op=ALU.mult),
                     reads=[("ps", 6 + tb), "cf"], writes=[dst_key])
            if defer:
                return part_b
            part_b()

        rot_h = Rot([0, 1, 2, 3])
        rot_o = Rot([(0, 1), (2, 3), (4, 5)])
        sgi = [0]

        def ffn(seq, half, w1d, w3d, w2d, gidx, store, do_norm=True, norm_next=False, mix_norm=False, next_seq_norm=False):
            tt0 = half * 8
            w1v, w3v = kv(w1d), kv(w3d)
            w2v = w2d.rearrange("(c p) d -> p c d", p=128)
            if do_norm:
                pq = []
                for t in range(8 + 3):
                    if t < 8:
                        pq.append(norm_tile(tt0 + t, hTf[:, :, t * 128:(t + 1) * 128], ("hTf", t), gidx, xn_f, defer=True))
                    if t >= 3:
                        pq.pop(0)()
            hkeys = [[("hTf", t) for t in range(th * 4, th * 4 + 4)] for th in range(2)]
            for cg in range(11):
                slot = cg % 2
                p.op("pool", lambda e, cg=cg, slot=slot: e.dma_start(out=w1s[slot][:, :, :], in_=w1v[:, :, cg * 256:(cg + 1) * 256]),
                     writes=[("w1b", slot)], dma=True)
                p.op("pool", lambda e, cg=cg, slot=slot: e.dma_start(out=w3s[slot][:, :, :], in_=w3v[:, :, cg * 256:(cg + 1) * 256]),
                     writes=[("w3b", slot)], dma=True)
                if cg in (1, 4):
                    pc = 0 if cg == 1 else 1
                    p.op("pool", lambda e, pc=pc: e.dma_start(out=w2sb[:, pc * 11:(pc + 1) * 11, :], in_=w2v[:, pc * 11:(pc + 1) * 11, :]),
                         writes=[("w2", pc)], dma=True)
                for cc in range(2):
                    c = cg * 2 + cc
                    for th in range(2):
                        b1 = rot_h.next()
                        b3 = rot_h.next()
                        for kc in range(8):
                            p.op("pe", lambda e, kc=kc, b1=b1, cc=cc, th=th, slot=slot: e.matmul(
                                psf[b1][:, :], lhsT=w1s[slot][:, kc, cc * 128:(cc + 1) * 128], rhs=hTf[:, kc, th * 512:(th + 1) * 512],
                                start=(kc == 0), stop=(kc == 7)),
                                reads=[("w1b", slot)] + hkeys[th], writes=PSK(b1))
                        for kc in range(8):
                            p.op("pe", lambda e, kc=kc, b3=b3, cc=cc, th=th, slot=slot: e.matmul(
                                psf[b3][:, :], lhsT=w3s[slot][:, kc, cc * 128:(cc + 1) * 128], rhs=hTf[:, kc, th * 512:(th + 1) * 512],
                                start=(kc == 0), stop=(kc == 7)),
                                reads=[("w3b", slot)] + hkeys[th], writes=PSK(b3))
                        si = sgi[0] % 2
                        sgi[0] += 1
                        p.op("act", lambda e, b1=b1, si=si: e.activation(out=sg[si][:, :], in_=psf[b1][:, :], func=AF.Silu),
                             reads=PSK(b1), writes=[("sg", si)])
                        p.op("dve", lambda e, b3=b3, si=si, c=c, th=th: e.tensor_tensor(
                            out=gT[:, c, th * 512:(th + 1) * 512], in0=sg[si][:, :], in1=psf[b3][:, :], op=ALU.mult),
                            reads=[("sg", si)] + PSK(b3), writes=[("gT", c, th)])
            if mix_norm:
                p.retire(["hTf", "w1b", "w3b"])
            due = {}
            for t in range(8):
                tt = tt0 + t
                for fb in due.pop(t, []):
                    fb()
                if norm_next:
                    due.setdefault(t + 1, []).append(norm_tile(tt0 + 8 + t, hTf[:, :, t * 128:(t + 1) * 128], ("hTf", t), gidx, xn_f, defer=True))
                if next_seq_norm:
                    due.setdefault(t + 1, []).append(norm_tile(t, hTf[:, :, t * 128:(t + 1) * 128], ("hTf", t), 0, xn_f, defer=True))
                if mix_norm:
                    due.setdefault(t + 1, []).append(norm_tile(t, hTm[:, :, t * 128:(t + 1) * 128], ("hTm", t), 1, xn_f, defer=True))
                banks = rot_o.next()
                for dh in range(2):
                    bk = banks[dh]
                    for c in range(NFC):
                        p.op("pe", lambda e, c=c, bk=bk, t=t, dh=dh: e.matmul(
                            psf[bk][:, :], lhsT=gT[:, c, t * 128:(t + 1) * 128], rhs=w2sb[:, c, dh * 512:(dh + 1) * 512],
                            start=(c == 0), stop=(c == NFC - 1)),
                            reads=[("gT", c, t // 4), ("w2", c // 11)], writes=PSK(bk))
                    p.op("dve", lambda e, bk=bk, tt=tt, dh=dh: e.scalar_tensor_tensor(
                        out=resid[:, tt, dh * 512:(dh + 1) * 512], in0=psf[bk][:, :], scalar=0.5,
                        in1=resid[:, tt, dh * 512:(dh + 1) * 512], op0=ALU.mult, op1=ALU.add),
                        reads=PSK(bk) + [("res", tt)], writes=[("res", tt)])
                if store:
                    g = seq * 16 + tt
                    p.op("sp", lambda e, g=g, tt=tt: e.dma_start(out=out_v[:, g, :], in_=resid[:, tt, :]),
                         reads=[("res", tt)], dma=True)
                if mix_norm:
                    tm = 8 + t
                    due.setdefault(t + 2, []).append(norm_tile(tm, hTm[:, :, tm * 128:(tm + 1) * 128], ("hTm", tm), 1, xn_f, defer=True))
            for k_ in sorted(due):
                for fb in due[k_]:
                    fb()

        FFN_NAMES = ["gT", "w2", "sg", "xn", "hTf", "w1b", "w3b"]
        ATT_NAMES = ["xn", "wqk", "wv", "qkT", "Va", "rden", "pt", "sq", "qn", "qr", "sqj", "rt"]
        HG_NAMES = ["wi", "wqfg", "vtm", "hA", "hB", "hC", "qd", "kd", "kk", "kktm", "Sbf", "silg", "hsq", "hrstd", "ht1", "attm", "hscm", "hhi", "hlo"]

        rot_full = Rot([0, 1, 2, 3, 4, 5])
        rot_half = Rot([(b, 0) for b in range(6)])
        rot_S = Rot([4, 5, 6, 7])
        pti = [0]
        evi = [0]

        def hslot(bh):
            b, h = bh
            return psf[b][:, h * 256:(h + 1) * 256]

        def attention_half(hf):
            allhT = [("hTm", tt) for tt in range(16)]
            def v_phase(bs):
              for b in bs:
                p.op("dve", lambda e, b=b: e.memset(Vaug[b][:, :, :, 64:128], 1.0), writes=[("Va", b, "ones")])
              for b in bs:
                Vv = Vaug[b].rearrange("p t a (k d) -> p t a k d", d=64)
                for tile in range(16):
                    if b == 0:
                        tok = lambda kc, tile=tile: hTm[:, kc, tile * 128:(tile + 1) * 128]
                        rk_ = [("hTm", tile)]
                    elif b == 1:
                        rho, j = tile // 4, tile % 4
                        s0 = 512 * j + rho
                        tok = lambda kc, s0=s0: hTm[:, kc, s0:s0 + 509:4]
                        rk_ = [("hTm", 4 * j + i) for i in range(4)]
                    else:
                        tok = lambda kc, tile=tile: hTm[:, kc, tile:tile + 2033:16]
                        rk_ = allhT
                    bh = rot_half.next()
                    for kc in range(8):
                        p.op("pe", lambda e, kc=kc, bh=bh, tok=tok: e.matmul(hslot(bh), lhsT=tok(kc), rhs=wv[:, kc, :], start=(kc == 0), stop=(kc == 7)),
                             reads=rk_ + [("wv", 0)], writes=PSK(*bh))
                    src = hslot(bh).rearrange("p (a k d) -> p a k d", a=2, k=2, d=64)
                    dst = Vv[:, tile, :, 0:3:2, :]
                    eng = "act" if evi[0] % 2 == 0 else "dve"
                    evi[0] += 1
                    if eng == "act":
                        p.op("act", lambda e, src=src, dst=dst: e.copy(out=dst, in_=src), reads=PSK(*bh), writes=[("Va", b, tile)])
                    else:
                        p.op("dve", lambda e, src=src, dst=dst: e.tensor_copy(out=dst, in_=src), reads=PSK(*bh), writes=[("Va", b, tile)])

            p.op("pool", lambda e: e.dma_start(out=wv[:, :, :], in_=win_v[:, :, 1024 + hf * 256:1024 + (hf + 1) * 256]), writes=[("wv", 0)], dma=True)
            p.op("pool", lambda e: e.dma_start(out=wqk[:, :, 0:256], in_=win_v[:, :, hf * 256:(hf + 1) * 256]), writes=[("wqk", 0)], dma=True)
            p.op("pool", lambda e: e.dma_start(out=wqk[:, :, 256:512], in_=win_v[:, :, 512 + hf * 256:512 + (hf + 1) * 256]), writes=[("wqk", 1)], dma=True)
            v_phase((0, 1))
            gq = cf[:, GQK:GQK + 512].rearrange("p (h d) -> p h d", d=64)
            qbank = {}

            def qk_mm(tt):
                bk = rot_full.next()
                qbank[tt] = bk
                for kc in range(8):
                    p.op("pe", lambda e, kc=kc: e.matmul(psf[bk][:, :], lhsT=hTm[:, kc, tt * 128:(tt + 1) * 128], rhs=wqk[:, kc, :],
                                                       start=(kc == 0), stop=(kc == 7)),
                         reads=[("hTm", tt), ("wqk", 0), ("wqk", 1)], writes=PSK(bk))

            def qk_Xa(tt):
                bk = qbank[tt]
                p.op("act", lambda e: e.activation(out=qn_a[:, :], in_=psf[bk][:, :], func=AF.Square, scale=0.125),
                     reads=PSK(bk), writes=["sqj"])

            def qk_Xb(tt):
                s4 = tt % 4
                p.op("dve", lambda e: e.tensor_reduce(out=ss8[:, s4, :], in_=qn_a.rearrange("p (h d) -> p h d", d=64), axis=AX.X, op=ALU.add),
                     reads=["sqj"], writes=[("ss8", s4)])
                p.op("act", lambda e: e.activation(out=rs8[:, s4, :], in_=ss8[:, s4, :], func=AF.Ln, bias=EPS, scale=1.0),
                     reads=[("ss8", s4)], writes=[("rs8", s4)])
                p.op("act", lambda e: e.activation(out=rs8[:, s4, :], in_=rs8[:, s4, :], func=AF.Exp, scale=-0.5),
                     reads=[("rs8", s4)], writes=[("rs8", s4)])

            def qk_Y(tt):
                bk = qbank[tt]
                sl = tt % 2
                qn_s, qr_s = qn2[sl], qr2[sl]
                ps3 = psf[bk].rearrange("p (h d) -> p h d", d=64)
                qn3 = qn_s.rearrange("p (h d) -> p h d", d=64)
                qr3 = qr_s.rearrange("p (h d) -> p h d", d=64)
                kqn = ("qn", sl)
                s4 = tt % 4
                p.op("dve", lambda e: e.tensor_tensor(out=qn3, in0=ps3, in1=rs8[:, s4, :].unsqueeze(2).to_broadcast([128, 8, 64]), op=ALU.mult),
                     reads=PSK(bk) + [("rs8", s4)], writes=[kqn])
                p.op("dve", lambda e: e.tensor_tensor(out=qn3, in0=qn3, in1=gq, op=ALU.mult), reads=[kqn, "cf"], writes=[kqn])
                cs_ = cf[:, COS + tt * 8:COS + tt * 8 + 8].unsqueeze(1).to_broadcast([128, 8, 8])
                sn_ = cf[:, SIN + tt * 8:SIN + tt * 8 + 8].unsqueeze(1).to_broadcast([128, 8, 8])
                x1 = qn3[:, :, 0:8]
                x2 = qn3[:, :, 8:16]
                T = [rt2[sl][:, i, :].rearrange("p (h d) -> p h d", d=8) for i in range(4)]
                rk = ("rt", sl)
                p.op("dve", lambda e: e.tensor_tensor(out=T[0], in0=x1, in1=cs_, op=ALU.mult), reads=[kqn, "cf"], writes=[rk + (0,)])
                p.op("dve", lambda e: e.tensor_tensor(out=T[1], in0=x2, in1=sn_, op=ALU.mult), reads=[kqn, "cf"], writes=[rk + (1,)])
                p.op("dve", lambda e: e.tensor_tensor(out=T[2], in0=x1, in1=sn_, op=ALU.mult), reads=[kqn, "cf"], writes=[rk + (2,)])
                p.op("dve", lambda e: e.tensor_tensor(out=T[3], in0=x2, in1=cs_, op=ALU.mult), reads=[kqn, "cf"], writes=[rk + (3,)])
                p.op("dve", lambda e: e.tensor_tensor(out=qr3[:, :, 0:8], in0=T[0], in1=T[1], op=ALU.subtract),
                     reads=[rk + (0,), rk + (1,)], writes=[("qr", sl, 0)])
                p.op("dve", lambda e: e.tensor_tensor(out=qr3[:, :, 8:16], in0=T[2], in1=T[3], op=ALU.add),
                     reads=[rk + (2,), rk + (3,)], writes=[("qr", sl, 1)])
                p.op("act", lambda e: e.copy(out=qr3[:, :, 16:64], in_=qn3[:, :, 16:64]), reads=[kqn], writes=[("qr", sl, 2)])
                tb = rot_T.next()
                for j in range(4):
                    p.op("pe", lambda e, j=j: e.transpose(out=psb[tb][:, j, :], in_=qr_s[:, j * 128:(j + 1) * 128], identity=identb),
                         reads=[("qr", sl, 0), ("qr", sl, 1), ("qr", sl, 2), "cb"], writes=[("ps", 6 + tb)])
                p.op("act", lambda e: e.copy(out=qkT[:, :, tt * 128:(tt + 1) * 128], in_=psb[tb][:, 0:4, :]),
                     reads=[("ps", 6 + tb)], writes=[("qkT", tt)])

            qk_mm(0)
            qk_mm(1)
            qk_mm(2)
            qk_Xa(0)
            qk_Xb(0)
            qk_Xa(1)
            qk_Xb(1)
            for tt in range(16):
                if tt + 3 < 16:
                    qk_mm(tt + 3)
                if tt + 2 < 16:
                    qk_Xa(tt + 2)
                qk_Y(tt)
                if tt + 2 < 16:
                    qk_Xb(tt + 2)
            p.retire(["qn", "qr", "rt", "sqj"])
            v_phase((2,))
            p.retire(["wqk", "wv", "qn", "qr"])
            allq = [("qkT", tt) for tt in range(16)]
            items = []
            for hh in range(4):
                h = 4 * hf + hh
                pair, odd = hh // 2, hh % 2
                base = 64 * odd
                qT = qkT[base:base + 64, pair, :]
                kT = qkT[base:base + 64, 2 + pair, :]
                va = lambda b, tile, pair=pair, odd=odd: Vaug[b][:, tile, pair, odd * 64:odd * 64 + 128]
                for g in range(8):
                    tiles = []
                    for j in (2 * g, 2 * g + 1):
                        nq = 256 if j < 15 else 128
                        pvs = []
                        for qb in range(nq // 128):
                            col0 = (j + qb) * 128
                            bank = col0 // 512
                            first = (j == 0 and qb == 0) or (qb == 1 and (j + 1) % 4 == 0)
                            pvs.append((bank, psf[bank][:, col0 % 512:col0 % 512 + 128], va(0, j), (qb * 128, (qb + 1) * 128), first, ("Va", 0, j)))
                        tiles.append((kT[:, j * 128:(j + 1) * 128], qT[:, j * 128:j * 128 + nq], nq,
                                      [("qkT", j)] + ([("qkT", j + 1)] if j < 15 else []), pvs))
                    items.append(("g", tiles))
                for rho in range(4):
                    for g in range(2):
                        tiles = []
                        for j in (2 * g, 2 * g + 1):
                            nq = 256 if j < 3 else 128
                            s0 = 512 * j + rho
                            pvs = []
                            for qb in range(nq // 128):
                                bank = j + qb
                                pvs.append((bank, psf[bank][:, rho:rho + 509:4], va(1, 4 * rho + j), (qb * 128, (qb + 1) * 128), False, ("Va", 1, 4 * rho + j)))
                            tiles.append((kT[:, s0:s0 + 509:4], qT[:, s0:s0 + 4 * (nq - 1) + 1:4], nq,
                                          [("qkT", 4 * j + i) for i in range(4 * (nq // 128))], pvs))
                        items.append(("g", tiles))
                for g in range(4):
                    tiles = []
                    for r in range(4 * g, 4 * g + 4):
                        pvs = [(bank, psf[bank][:, r:r + 497:16], va(2, r), (bank * 32, (bank + 1) * 32), False, ("Va", 2, r)) for bank in range(4)]
                        tiles.append((kT[:, r:r + 2033:16], qT[:, r:r + 2033:16], 128, allq, pvs))
                    items.append(("g", tiles))
                items.append(("fin", h, odd))

            def stage_S(tiles, state):
                sbk = rot_S.next()
                k = pti[0] % 6
                pti[0] += 1
                state["k"] = k
                pS = psf[sbk]
                col = 0
                offs = []
                for (kap, qap, nq, rkeys, pvs) in tiles:
                    p.op("pe", lambda e, kap=kap, qap=qap, col=col, nq=nq: e.matmul(pS[:, col:col + nq], lhsT=kap, rhs=qap, start=True, stop=True),
                         reads=rkeys, writes=PSK(sbk))
                    offs.append(col)
                    col += nq
                tot = col
                state["offs"] = offs
                p.op("act", lambda e: e.activation(out=pt[k][:, 0:tot], in_=pS[:, 0:tot], func=AF.Exp, scale=0.125),
                     reads=PSK(sbk), writes=[("pt", k)])
                nqs = [t[2] for t in tiles]
                if tot == 512 and len(set(nqs)) == 1:
                    n, w = len(nqs), nqs[0]
                    mv = cb[:, MSK:MSK + w].unsqueeze(1).to_broadcast([128, n, w])
                    pv_ = pt[k][:, :].rearrange("p (n w) -> p n w", w=w)
                    p.op("dve", lambda e: e.tensor_tensor(out=pv_, in0=pv_, in1=mv, op=ALU.mult), reads=[("pt", k), "cb"], writes=[("pt", k)])
                else:
                    for off, w in zip(offs, nqs):
                        p.op("dve", lambda e, off=off, w=w: e.tensor_tensor(out=pt[k][:, off:off + w], in0=pt[k][:, off:off + w],
                                                                          in1=cb[:, MSK:MSK + w], op=ALU.mult),
                             reads=[("pt", k), "cb"], writes=[("pt", k)])

            def stage_PV(tiles, state):
                k = state["k"]
                for (kap, qap, nq, rkeys, pvs), off in zip(tiles, state["offs"]):
                    for (bank, oap, lhs, (a, b_), first, vkey) in pvs:
                        p.op("pe", lambda e, oap=oap, lhs=lhs, a=a, b_=b_, off=off, first=first: e.matmul(
                            oap, lhsT=lhs, rhs=pt[k][:, off + a:off + b_], start=first, stop=False, skip_group_check=True),
                            reads=[vkey, ("pt", k), ("Va", vkey[1], "ones")], writes=PSK(bank))

            def stage_fin(h, odd):
                for bank in range(4):
                    if odd == 0:
                        nsl, dsl = slice(0, 64), slice(64, 128)
                    else:
                        nsl, dsl = slice(64, 128), slice(0, 64)
                    p.op("act", lambda e, bank=bank, nsl=nsl, dsl=dsl: e.activation(out=rden[nsl, :], in_=psf[bank][dsl, :], func=AF.Ln),
                         reads=PSK(bank), writes=["rden"])
                    p.op("act", lambda e, nsl=nsl: e.activation(out=rden[nsl, :], in_=rden[nsl, :], func=AF.Exp, scale=-1.0),
                         reads=["rden"], writes=["rden"])
                    p.op("dve", lambda e, bank=bank, nsl=nsl, h=h: e.tensor_tensor(
                        out=mixedT[nsl, h // 2, bank * 512:(bank + 1) * 512], in0=psf[bank][nsl, :], in1=rden[nsl, :], op=ALU.mult),
                        reads=PSK(bank) + ["rden"], writes=[("mixedT", h, bank)])

            L = 3
            states = [dict() for _ in items]
            for idx in range(len(items) + L):
                if idx < len(items) and items[idx][0] == "g":
                    stage_S(items[idx][1], states[idx])
                if idx >= L:
                    it = items[idx - L]
                    if it[0] == "g":
                        stage_PV(it[1], states[idx - L])
                    else:
                        stage_fin(it[1], it[2])

        hbi = [0]
        ami = [0]
        rot_hb = Rot([5, 7])
        hgc = [0]

        def hgrn():
            p.op("pool", lambda e: e.dma_start(out=wi[:, :, :], in_=win_v[:, :, 2560:3072]), writes=[("wi", 0)], dma=True)
            for tt in range(16):
                bk = rot_full.next()
                for kc in range(8):
                    p.op("pe", lambda e, kc=kc, bk=bk, tt=tt: e.matmul(psf[bk][:, :], lhsT=hTm[:, kc, tt * 128:(tt + 1) * 128], rhs=wi[:, kc, :],
                                                                     start=(kc == 0), stop=(kc == 7)),
                         reads=[("hTm", tt), ("wi", 0)], writes=PSK(bk))
                if tt % 2 == 0:
                    p.op("act", lambda e, bk=bk, tt=tt: e.copy(out=vtm[:, tt, :], in_=psf[bk][:, :]), reads=PSK(bk), writes=[("vtm", tt)])
                else:
                    p.op("dve", lambda e, bk=bk, tt=tt: e.tensor_copy(out=vtm[:, tt, :], in_=psf[bk][:, :]), reads=PSK(bk), writes=[("vtm", tt)])
            p.op("dve", lambda e: e.memset(hscm[:, :], 1.0), writes=["hscm"])
            p.op("dve", lambda e: e.memset(hscm[:, 0:512:64], 0.0), writes=["hscm"])
            blocks = [(h, cbk) for h in range(4) for cbk in range(4)]
            sts = [dict() for _ in blocks]
            for bi_, st_ in enumerate(sts):
                st_["sl"] = hbi[0] % 2
                st_["sg"] = bi_ % 3
                st_["bo"] = 3 + (bi_ % 2)
                hbi[0] += 1
            def interleave(gens):
                live = [[g, p.eng_free["pe"] * 0.0] for (g, w) in gens]
                while live:
                    it = min(live, key=lambda x: x[1])
                    try:
                        next(it[0])
                        it[1] = p.last_end
                    except StopIteration:
                        live.remove(it)

            nb_ = len(blocks)
            interleave([(hg_front(0, 0, sts[0]), 1)])
            p.retire(["wi"])
            p.op("pool", lambda e: e.dma_start(out=woA[:, :, :], in_=wout_v[:, :, 0:512]), writes=[("woA", 0)], dma=True)
            for i in range(nb_ + 1):
                gens = []
                if i + 1 < nb_:
                    gens.append((hg_front(blocks[i + 1][0], blocks[i + 1][1], sts[i + 1]), 2))
                if i < nb_:
                    gens.append((hg_back(blocks[i][0], blocks[i][1], sts[i]), 3))
                if i >= 1:
                    gens.append((hg_out(blocks[i - 1][0], blocks[i - 1][1], sts[i - 1]), 1))
                interleave(gens)

        def hg_front(h, cbk, stt):
            wsl = h % 2
            if cbk == 0:
                for i, c0 in enumerate((1536, 2048, 3072)):
                    p.op("pool", lambda e, i=i, c0=c0: e.dma_start(out=wqfg2[wsl][:, :, i, :], in_=win_v[:, :, c0 + h * 128:c0 + (h + 1) * 128]),
                         writes=[("wqfg", wsl, i)], dma=True)
                    yield
            lb_h = lbt[:, h:h + 1]
            oml_h = lbt[:, 4 + h:5 + h]
            sl = stt["sl"]
            sgs = stt["sg"]
            cols = slice(cbk * 512, (cbk + 1) * 512)
            hkeys = [("hTm", cbk * 4 + i) for i in range(4)]
            bq, bf_, bg = 0, 1, 2
            for i, bk in ((1, bf_), (2, bg), (0, bq)):
                for kc in range(8):
                    p.op("pe", lambda e, kc=kc, bk=bk, i=i: e.matmul(psf[bk][:, :], lhsT=wqfg2[wsl][:, kc, i, :], rhs=hTm[:, kc, cols],
                                                                   start=(kc == 0), stop=(kc == 7)),
                         reads=hkeys + [("wqfg", wsl, i)], writes=PSK(bk), dur=0.3)
                yield
            A, B, C = hA[sl], hB[sl], hC[sl]
            p.op("act", lambda e: e.activation(out=A[:, :], in_=psf[bf_][:, :], func=AF.Sigmoid), reads=PSK(bf_), writes=[("hA", sl)], aset="sig")
            yield
            p.op("act", lambda e: e.activation(out=B[:, :], in_=psf[bf_][:, :], func=AF.Sigmoid, scale=-1.0), reads=PSK(bf_), writes=[("hB", sl)], aset="sig")
            yield
            p.op("act", lambda e: e.activation(out=silg[sgs][:, :], in_=psf[bg][:, :], func=AF.Sigmoid), reads=PSK(bg), writes=[("silg", sgs)], aset="sig")
            yield
            p.op("dve", lambda e: e.tensor_tensor(out=silg[sgs][:, :], in0=psf[bg][:, :], in1=silg[sgs][:, :], op=ALU.mult),
                 reads=PSK(bg) + [("silg", sgs)], writes=[("silg", sgs)])
            yield
            p.op("act", lambda e: e.activation(out=A[:, :], in_=A[:, :], func=AF.Ln, scale=oml_h, bias=lb_h),
                 reads=[("hA", sl), "lbt"], writes=[("hA", sl)], aset="le")
            yield
            p.op("dve", lambda e: e.tensor_tensor_scan(out=C[:, :], data0=hscm[:, :], data1=A[:, :], initial=0.0, op0=ALU.mult, op1=ALU.add),
                 reads=[("hA", sl), "hscm"], writes=[("hC", sl)], dur=1.25)
            yield
            p.op("act", lambda e: e.activation(out=A[:, :], in_=C[:, :], func=AF.Exp), reads=[("hC", sl)], writes=[("hA", sl)], aset="le")
            yield
            p.op("act", lambda e: e.activation(out=C[:, :], in_=C[:, :], func=AF.Exp, scale=-1.0), reads=[("hC", sl)], writes=[("hC", sl)], aset="le")
            yield
            p.op("dve", lambda e: e.tensor_tensor(out=qd[sl][:, :], in0=psf[bq][:, :], in1=A[:, :], op=ALU.mult),
                 reads=PSK(bq) + [("hA", sl)], writes=[("qd", sl)])
            yield
            p.op("dve", lambda e: e.scalar_tensor_tensor(out=kd[sl][:, :], in0=B[:, :], scalar=oml_h, in1=C[:, :], op0=ALU.mult, op1=ALU.mult),
                 reads=[("hB", sl), ("hC", sl), "lbt"], writes=[("kd", sl)])
            yield
            dec = A[:, 63:512:64]
            p.op("dve", lambda e: e.tensor_tensor(out=kk[sl].rearrange("p (c t) -> p c t", t=64),
                                                  in0=kd[sl].rearrange("p (c t) -> p c t", t=64),
                                                  in1=dec.unsqueeze(2).to_broadcast([128, 8, 64]), op=ALU.mult),
                 reads=[("kd", sl), ("hA", sl)], writes=[("kk", 0)])
            yield
            tb = 0
            for i in range(4):
                p.op("pe", lambda e, i=i: e.transpose(out=psb[tb][:, i, :], in_=kk[sl][:, i * 128:(i + 1) * 128], identity=identb),
                     reads=[("kk", 0), "cb"], writes=[("ps", 6 + tb)])
            p.op("act", lambda e: e.copy(out=kktm[sl][:, :, :], in_=psb[tb][:, 0:4, :]), reads=[("ps", 6 + tb)], writes=[("kktm", sl)])
            yield

        def hg_back(h, cbk, stt):
            sl = stt["sl"]
            A = hA[sl]
            cols = slice(cbk * 512, (cbk + 1) * 512)
            if cbk == 0:
                p.op("dve", lambda e: e.memset(Sf[:, 0, :], 0.0), writes=[("Sf", 0)])
                yield
                p.op("dve", lambda e: e.memset(Sbf[sl][:, 0, :], 0.0), writes=[("Sbf", sl, 0)])
                yield
                hgc[0] = 0
            bo = stt["bo"]
            pend = None
            for i in range(4):
                tt = cbk * 4 + i
                ba = rot_hb.next()
                while ba == bo:
                    ba = rot_hb.next()
                am = ami[0] % 4
                ami[0] += 1
                p.op("pe", lambda e, i=i, ba=ba: e.matmul(psf[ba][:, 0:128], lhsT=kd[sl][:, i * 128:(i + 1) * 128], rhs=qd[sl][:, i * 128:(i + 1) * 128],
                                                         start=True, stop=True),
                     reads=[("kd", sl), ("qd", sl)], writes=PSK(ba))
                p.op("dve", lambda e, ba=ba, am=am: e.tensor_tensor(out=attm[am][:, :], in0=psf[ba][:, 0:128], in1=hmask, op=ALU.mult),
                     reads=PSK(ba) + ["cb"], writes=[("attm", am)], dur=0.3)
                yield
                for cc in range(2):
                    c = i * 2 + cc
                    bc = rot_hb.next()
                    while bc == bo:
                        bc = rot_hb.next()
                    p.op("pe", lambda e, i=i, cc=cc, bc=bc, tt=tt: e.matmul(
                        psf[bc][:, 0:128], lhsT=kktm[sl][cc * 64:(cc + 1) * 64, i, :], rhs=vtm[cc * 64:(cc + 1) * 64, tt, h * 128:(h + 1) * 128],
                        start=True, stop=True),
                        reads=[("kktm", sl), ("vtm", tt)], writes=PSK(bc))
                    so, sn = hgc[0] % 2, (hgc[0] + 1) % 2
                    hgc[0] += 1
                    p.op("dve", lambda e, so=so, sn=sn, c=c, bc=bc: e.scalar_tensor_tensor(
                        out=Sf[:, sn, :], in0=Sf[:, so, :], scalar=A[:, c * 64 + 63:c * 64 + 64], in1=psf[bc][:, 0:128],
                        op0=ALU.mult, op1=ALU.add),
                        reads=[("Sf", so), ("hA", sl)] + PSK(bc), writes=[("Sf", sn)], dur=0.35)
                    yield
                    if c < 7:
                        dkey = ("Sbf", sl, c + 1)
                        dap = Sbf[sl][:, c + 1, :]
                    else:
                        dkey = ("Sbf", 1 - sl, 0)
                        dap = Sbf[1 - sl][:, 0, :]
                    p.op("act", lambda e, sn=sn, dap=dap: e.copy(out=dap, in_=Sf[:, sn, :]), reads=[("Sf", sn)], writes=[dkey], dur=0.3)
                    yield

                def emit_po(i=i, tt=tt, am=am):
                    oap = psf[bo][:, i * 128:(i + 1) * 128]
                    okey = PSK(bo)
                    p.op("pe", lambda e: e.matmul(oap, lhsT=vtm[:, tt, h * 128:(h + 1) * 128], rhs=attm[am][:, :], start=True, stop=False,
                                                  skip_group_check=True),
                         reads=[("vtm", tt), ("attm", am)], writes=okey)
                    for cc in range(2):
                        c = i * 2 + cc
                        p.op("pe", lambda e, cc=cc, c=c: e.matmul(oap[:, cc * 64:(cc + 1) * 64], lhsT=Sbf[sl][:, c, :],
                                                                 rhs=qd[sl][:, i * 128 + cc * 64:i * 128 + (cc + 1) * 64],
                                                                 start=False, stop=(cc == 1), skip_group_check=True),
                             reads=[("Sbf", sl, c), ("qd", sl)], writes=okey)
                    yield
                if pend is not None:
                    yield from pend()
                pend = emit_po
            yield from pend()

        def hg_out(h, cbk, stt):
            sl = stt["sl"]
            sgs = stt["sg"]
            bo = stt["bo"]
            cols = slice(cbk * 512, (cbk + 1) * 512)
            p.op("act", lambda e: e.activation(out=hsq[:, :], in_=psf[bo][:, :], func=AF.Square), reads=PSK(bo), writes=["hsq"])
            yield
            bs = rot_hb.next()
            while bs == bo:
                bs = rot_hb.next()
            p.op("act", lambda e: e.copy(out=hhi[:, :], in_=hsq[:, :]), reads=["hsq"], writes=["hhi"])
            yield
            p.op("dve", lambda e: e.tensor_tensor(out=hlo[:, :], in0=hsq[:, :], in1=hhi[:, :], op=ALU.subtract),
                 reads=["hsq", "hhi"], writes=["hlo"])
            yield
            p.op("pe", lambda e: e.matmul(psf[bs][:, :], lhsT=cb[:, ONEB:ONEB + 128], rhs=hhi[:, :], start=True, stop=False),
                 reads=["hhi", "cb"], writes=PSK(bs))
            p.op("pe", lambda e: e.matmul(psf[bs][:, :], lhsT=cb[:, ONEB:ONEB + 128], rhs=hlo[:, :], start=False, stop=True),
                 reads=["hlo", "cb"], writes=PSK(bs))
            p.op("act", lambda e: e.activation(out=hrstd[:, :], in_=psf[bs][:, :], func=AF.Ln, bias=EPS, scale=1.0),
                 reads=PSK(bs), writes=["hrstd"], aset="le")
            yield
            p.op("act", lambda e: e.activation(out=hrstd[:, :], in_=hrstd[:, :], func=AF.Exp, scale=-0.5), reads=["hrstd"], writes=["hrstd"], aset="le")
            yield
            p.op("dve", lambda e: e.scalar_tensor_tensor(out=hsq[:, :], in0=psf[bo][:, :], scalar=cf[:, GOUT:GOUT + 1], in1=hrstd[:, :],
                                                         op0=ALU.mult, op1=ALU.mult),
                 reads=PSK(bo) + ["hrstd", "cf", "hhi", "hlo"], writes=["hsq"])
            yield
            p.op("dve", lambda e: e.tensor_tensor(out=mixedT[:, 4 + h, cols], in0=hsq[:, :], in1=silg[sgs][:, :], op=ALU.mult),
                 reads=["hsq", ("silg", sgs)], writes=[("mixedT", 8 + h, cbk)])
            yield

        def wout(ffn_norm=False):
            p.op("pool", lambda e: e.dma_start(out=woB[:, :, :], in_=wout_v[:, :, 512:1024]), writes=[("woB", 0)], dma=True)
            mkeys = [("mixedT", h, b) for h in range(8) for b in range(4)] + [("mixedT", 8 + h, b) for h in range(4) for b in range(4)]
            due = {}
            for dh, (wsb, wkey) in enumerate(((woA, ("woA", 0)), (woB, ("woB", 0)))):
                for tt in range(16):
                    g_ = dh * 16 + tt
                    for fb in due.pop(g_, []):
                        fb()
                    bk = rot_full.next()
                    for kc in range(8):
                        if kc < 4:
                            mk = [("mixedT", 2 * kc, tt // 4), ("mixedT", 2 * kc + 1, tt // 4)]
                        else:
                            mk = [("mixedT", 8 + kc - 4, tt // 4)]
                        p.op("pe", lambda e, kc=kc, bk=bk, tt=tt, wsb=wsb: e.matmul(
                            psf[bk][:, :], lhsT=mixedT[:, kc, tt * 128:(tt + 1) * 128], rhs=wsb[:, kc, :],
                            start=(kc == 0), stop=(kc == 7)),
                            reads=mk + [wkey], writes=PSK(bk))
                    p.op("dve", lambda e, bk=bk, tt=tt, dh=dh: e.tensor_tensor(
                        out=resid[:, tt, dh * 512:(dh + 1) * 512], in0=psf[bk][:, :], in1=resid[:, tt, dh * 512:(dh + 1) * 512], op=ALU.add),
                        reads=PSK(bk) + [("res", tt)], writes=[("res", tt)])
                    if ffn_norm and dh == 1 and tt < 8:
                        due.setdefault(g_ + 2, []).append(norm_tile(tt, hTf[:, :, tt * 128:(tt + 1) * 128], ("hTf", tt), 2, xn_f, defer=True))
            for k_ in sorted(due):
                for fb in due[k_]:
                    fb()

        def load_x(seq, g0, g1):
            for g in range(g0, g1):
                p.op("sp", lambda e, g=g, seq=seq: e.dma_start(out=resid[:, 2 * g:2 * g + 2, :], in_=x_v[:, seq * 16 + 2 * g:seq * 16 + 2 * g + 2, :]),
                     writes=[("res", 2 * g), ("res", 2 * g + 1)], dma=True)

        chain = (stage >= 3)
        for seq in range(nseq):
            first = (seq == 0) or not chain
            if first:
                load_x(seq, 0, 8)
            ffn(seq, 0, w1a, w3a, w2a, 0, store=(stage == 1), do_norm=first, norm_next=True)
            ffn(seq, 1, w1a, w3a, w2a, 0, store=(stage == 1), do_norm=False, mix_norm=(stage >= 2))
            p.retire(FFN_NAMES)
            if stage >= 2:
                if dbg and seq == 0:
                    p.op("sp", lambda e: e.dma_start(out=dbg_hT[:, :, :], in_=hTm[:, :, :]), reads=[("hTm", tt) for tt in range(16)], dma=True)
                p.retire(["xn"])
                for hf in range(2):
                    if os.environ.get("SKIP_ATT"):
                        continue
                    attention_half(hf)
                    p.retire(ATT_NAMES)
                if not os.environ.get("SKIP_HG"):
                    hgrn()
                p.retire(HG_NAMES + ["hTm"])
                if dbg and seq == 0:
                    p.op("sp", lambda e: e.dma_start(out=dbg_mixed[:, :, :], in_=mixedT[:, :, :]),
                         reads=[("mixedT", h, b) for h in range(12) for b in range(4)], dma=True)
                wout(ffn_norm=(stage >= 3))
                p.retire(["woA", "woB", "mixedT"])
                if stage == 2:
                    for tt in range(16):
                        p.op("sp", lambda e, tt=tt, seq=seq: e.dma_start(out=out_v[:, seq * 16 + tt, :], in_=resid[:, tt, :]),
                             reads=[("res", tt)], dma=True)
            if stage >= 3:
                ffn(seq, 0, w1b_, w3b_, w2b_, 2, store=True, do_norm=False, norm_next=True)
                more = chain and seq + 1 < nseq
                if more:
                    load_x(seq + 1, 0, 4)
                ffn(seq, 1, w1b_, w3b_, w2b_, 2, store=True, do_norm=False, next_seq_norm=more)
                if more:
                    load_x(seq + 1, 4, 8)
                else:
                    p.retire(FFN_NAMES)
            p.retire(["hTm"])
        p.emit(st)
        build.stats = {e: len([o for o in p.ops if o.eng == e]) for e in ENGS}
        build.maxcnt = p.max_cnt
    return nc


def make_consts(inputs):
    f32 = np.float32
    cf = np.zeros((128, NCF), f32)
    pidx = np.arange(128)
    for n, name in enumerate(("ffn1_norm", "mix_norm", "ffn2_norm")):
        g = np.asarray(inputs[name], f32).reshape(8, 128)
        cf[:, GAM + n * 8:GAM + n * 8 + 8] = g.T
    inv = (f32(500000.0) ** (-(np.arange(0, 16, 2, dtype=f32)) / f32(16))).astype(f32)
    pos = (np.arange(16)[None, :] * 128 + pidx[:, None]).astype(f32)
    ang = (pos[:, :, None] * inv[None, None, :]).astype(f32)
    cf[:, COS:COS + 128] = np.cos(ang).astype(f32).reshape(128, 128)
    cf[:, SIN:SIN + 128] = np.sin(ang).astype(f32).reshape(128, 128)
    qn = np.asarray(inputs["q_norm"], f32).reshape(64)
    kn = np.asarray(inputs["k_norm"], f32).reshape(64)
    cf[:, GQK:GQK + 512] = np.concatenate([np.tile(qn, 4), np.tile(kn, 4)])[None, :]
    lbl = np.asarray(inputs["hg_lb_logits"], f32).reshape(2, 4, 128)
    cf[:, LBL:LBL + 8] = lbl.transpose(2, 0, 1).reshape(128, 8)
    cf[:, GOUT] = np.asarray(inputs["hg_out_norm"], f32).reshape(128)
    cf[:, ONESF:ONESF + 128] = 1.0 / 128.0
    scm = np.ones(64, f32)
    scm[0] = 0.0
    cf[:, SCM:SCM + 64] = scm[None, :]
    cb = np.zeros((128, NCB), f32)
    cb[:, IDB:IDB + 128] = np.eye(128, dtype=f32)
    cb[:, ONEB:ONEB + 128] = 1.0 / 128.0
    s_ = pidx[:, None]
    t_ = np.arange(256)[None, :]
    cb[:, MSK:MSK + 256] = ((t_ - s_ >= 0) & (t_ - s_ <= 128)).astype(f32)
    t2 = np.arange(128)[None, :]
    cb[:, HMK:HMK + 128] = ((s_ // 64 == t2 // 64) & (s_ <= t2)).astype(f32)
    return cf, cb.astype(ml_dtypes.bfloat16)


_NC_CACHE = {}


def run(inputs, stage=3, dbg=False, cores=NCORES, trace=False):
    key = (stage, dbg)
    if key not in _NC_CACHE:
        _NC_CACHE[key] = build(stage, dbg)
    nc = _NC_CACHE[key]
    cf, cb = make_consts(inputs)
    x = np.ascontiguousarray(np.asarray(inputs["x"], np.float32)).reshape(NCORES, TOK, D)
    shared = {
        "cf": cf, "cb": cb,
        "w_in": np.ascontiguousarray(np.asarray(inputs["w_in"], np.float32)[0]),
        "w_out": np.ascontiguousarray(np.asarray(inputs["w_out"], np.float32)[0]),
    }
    for n in ("ffn1_w1", "ffn1_w3", "ffn1_w2", "ffn2_w1", "ffn2_w3", "ffn2_w2"):
        shared[n] = np.ascontiguousarray(np.asarray(inputs[n], np.float32)[0])
    in_maps = [dict(shared, x=x[i]) for i in range(cores)]
    res = run_bass_kernel_spmd(nc, in_maps, core_ids=list(range(cores)), **({"trace": True} if trace else {}))
    return res


def kernel(**inputs):
    res = run(inputs)
    out = np.stack([np.asarray(r["out"], np.float32) for r in res.results], axis=0)
    return out.reshape(16, S, D)
```

```python
import os
import numpy as np
import ml_dtypes
from contextlib import ExitStack
import concourse.bass as bass
import concourse.mybir as mybir
from concourse.bass_utils import run_bass_kernel_spmd

F32 = mybir.dt.float32
BF16 = mybir.dt.bfloat16
AF = mybir.ActivationFunctionType
ALU = mybir.AluOpType
AX = mybir.AxisListType
EPS = 1e-6

NCORES = 8
S = 2048
D = 1024
DFF = 2816
NFC = 22
TOK = 2 * S

GAM, COS, SIN, GQK, LBL, GOUT, ONESF, SCM = 0, 24, 152, 280, 792, 800, 801, 929
NCF = 996
IDB, MSK, HMK, ONEB = 0, 128, 384, 512
NCB = 640

ARENA_BYTES = 105472

ENGS = ("pe", "act", "dve", "pool", "sp")


class Op:
    __slots__ = ("eng", "fn", "dma", "deps", "flag", "cnt", "dsem", "dval", "idx", "tend")

    def __init__(self, eng, fn, dma):
        self.eng = eng
        self.fn = fn
        self.dma = dma
        self.deps = []
        self.flag = False
        self.cnt = 0
        self.dsem = None
        self.dval = 0


class Prog:
    def __init__(self, nc, ndma_sems=16):
        self.nc = nc
        self.ops = []
        self.last_w = {}
        self.readers = {}
        self.ndma = ndma_sems
        self.pending = {}
        self.pending_dma = []
        self.eng_free = {e: 0.0 for e in ENGS}
        self.last_end = 0.0

    def retire(self, names):
        names = set(names)
        dead = [k for k in list(self.last_w.keys()) + list(self.readers.keys())
                if (k[0] if isinstance(k, tuple) else k) in names]
        for k in set(dead):
            ops = []
            w = self.last_w.pop(k, None)
            if w is not None:
                ops.append(w)
            ops.extend(self.readers.pop(k, ()))
            for o in ops:
                if o.dma:
                    if o not in self.pending_dma:
                        self.pending_dma.append(o)
                else:
                    cur = self.pending.get(o.eng)
                    if cur is None or cur.idx < o.idx:
                        self.pending[o.eng] = o

    def op(self, eng, fn, reads=(), writes=(), dma=False, dur=None, aset=None):
        o = Op(eng, fn, dma)
        o.idx = len(self.ops)
        if dur is None:
            dur = 2.0 if dma else (0.12 if eng == "pe" else (0.63 if eng == "act" else 0.7))
        if aset is not None:
            if getattr(self, "cur_aset", None) not in (None, aset):
                dur += 1.3
            self.cur_aset = aset
        deps = set()
        for k in reads:
            w = self.last_w.get(k)
            if w is not None:
                deps.add(w)
        for k in writes:
            if k not in self.last_w and k not in self.readers:
                deps.update(self.pending.values())
                deps.update(self.pending_dma)
            w = self.last_w.get(k)
            if w is not None:
                deps.add(w)
            for r in self.readers.get(k, ()):
                deps.add(r)
        for k in writes:
            self.last_w[k] = o
            self.readers[k] = []
        for k in reads:
            self.readers.setdefault(k, []).append(o)
        deps.discard(o)
        for d in deps:
            if (not d.dma) and (not o.dma) and d.eng == o.eng and o.eng == "pe":
                continue
            o.deps.append(d)
        t0 = self.eng_free[eng] if not dma else 0.0
        for d in deps:
            if d.tend + 0.1 > t0:
                t0 = d.tend + 0.1
        o.tend = t0 + dur
        if not dma:
            self.eng_free[eng] = o.tend
        self.last_end = o.tend
        self.ops.append(o)
        return o

    def emit(self, stack):
        nc = self.nc
        per = {e: [o for o in self.ops if o.eng == e] for e in ENGS}
        for o in self.ops:
            for d in o.deps:
                if not d.dma:
                    d.flag = True
        esem = {e: stack.enter_context(nc.semaphore("s_" + e)) for e in ENGS if e != "sp"}
        for e in ENGS:
            c = 0
            for o in per[e]:
                if not o.dma and o.flag:
                    c += 1
                    o.cnt = c
        extra_wait = {}
        for e in ENGS:
            dl = [o for o in per[e] if o.dma]
            if not dl:
                continue
            sems = [stack.enter_context(nc.semaphore("d_%s%d" % (e, i))) for i in range(self.ndma)]
            for j, o in enumerate(dl):
                o.dsem = sems[j % self.ndma]
                o.dval = 16 * (j // self.ndma + 1)
                if j >= self.ndma:
                    extra_wait[o] = dl[j - self.ndma]
        self.max_cnt = {e: max([o.cnt for o in per[e]] + [0]) for e in ENGS}
        block = stack.enter_context(nc.Block())
        handles = {"pe": block.tensor, "act": block.scalar, "dve": block.vector,
                   "pool": block.gpsimd, "sp": block.sync}

        def run(e, eng):
            waited = {}
            for o in per[e]:
                deps = list(o.deps)
                if o in extra_wait:
                    deps.append(extra_wait[o])
                need = {}
                for d in deps:
                    if d.dma:
                        s, v = d.dsem, d.dval
                    else:
                        s, v = esem[d.eng], d.cnt
                    key = id(s)
                    if waited.get(key, 0) >= v:
                        continue
                    if key not in need or need[key][1] < v:
                        need[key] = (s, v)
                for key, (s, v) in need.items():
                    eng.wait_ge(s, v)
                    waited[key] = v
                ins = o.fn(eng)
                if o.dma:
                    ins.then_inc(o.dsem, 16)
                elif o.flag:
                    ins.then_inc(esem[e], 1)
            last = {}
            for o in per[e]:
                if o.dma:
                    last[id(o.dsem)] = (o.dsem, o.dval)
            for key, (s, v) in last.items():
                if waited.get(key, 0) < v:
                    eng.wait_ge(s, v)

        for e in ENGS:
            if per[e]:
                handles[e]((lambda ee: (lambda eng: run(ee, eng)))(e))


class Rot:
    def __init__(self, items):
        self.items = list(items)
        self.i = 0

    def next(self):
        it = self.items[self.i % len(self.items)]
        self.i += 1
        return it


def build(stage=3, dbg=False, nseq=2):
    nc = bass.Bass("TRN2", target_bir_lowering=False)

    def din(name, shape, dt=F32):
        return nc.dram_tensor(name, shape, dt, kind="ExternalInput").ap()

    x_d = din("x", [TOK, D])
    out_d = nc.dram_tensor("out", [TOK, D], F32, kind="ExternalOutput").ap()
    w1a = din("ffn1_w1", [D, DFF]); w3a = din("ffn1_w3", [D, DFF]); w2a = din("ffn1_w2", [DFF, D])
    w1b_ = din("ffn2_w1", [D, DFF]); w3b_ = din("ffn2_w3", [D, DFF]); w2b_ = din("ffn2_w2", [DFF, D])
    win_d = din("w_in", [D, 3584]); wout_d = din("w_out", [D, D])
    cf_d = din("cf", [128, NCF]); cb_d = din("cb", [128, NCB], BF16)
    if dbg:
        dbg_mixed = nc.dram_tensor("dbg_mixed", [128, 8, S], BF16, kind="ExternalOutput").ap()
        dbg_hT = nc.dram_tensor("dbg_hT", [128, 8, S], BF16, kind="ExternalOutput").ap()

    x_v = x_d.rearrange("(t p) d -> p t d", p=128)
    out_v = out_d.rearrange("(t p) d -> p t d", p=128)
    kv = lambda w: w.rearrange("(k p) n -> p k n", p=128)
    win_v = kv(win_d)
    wout_v = kv(wout_d)

    with ExitStack() as st:
        sb = lambda name, shape, dt: st.enter_context(nc.sbuf_tensor(name, shape, dt))
        resid = sb("resid", [128, 16, D], F32)
        hTb = sb("hTb", [128, 16384], BF16)
        cf = sb("cf_sb", [128, NCF], F32)
        cb = sb("cb_sb", [128, NCB], BF16)
        ms = sb("ms", [128, 96], F32)
        rs = sb("rs", [128, 96], F32)
        lbt = sb("lbt", [128, 8], F32)
        ss8 = sb("ss8", [128, 4, 8], F32)
        rs8 = sb("rs8", [128, 4, 8], F32)
        Sf = sb("Sf", [128, 2, 128], F32)
        arena = sb("arena", [128, ARENA_BYTES // 2], BF16)
        psf = [st.enter_context(nc.psum_tensor("ps%d" % i, [128, 512], F32)) for i in range(8)]
        psb = [psf[6 + i][:, :].bitcast(BF16).rearrange("p (k t) -> p k t", t=128) for i in range(2)]

        def carve(off, shape, dt):
            n = int(np.prod(shape[1:]))
            nb = n * (4 if dt == F32 else 2)
            assert off % 4 == 0 and off + nb <= ARENA_BYTES, (off, shape)
            a = arena[:, off // 2:(off + nb) // 2]
            if dt == F32:
                a = a.bitcast(F32)
            if len(shape) == 3:
                a = a.rearrange("p (a b) -> p a b", b=shape[2])
            elif len(shape) == 4:
                a = a.rearrange("p (a b c) -> p a b c", b=shape[2], c=shape[3])
            elif len(shape) == 5:
                a = a.rearrange("p (a b c d) -> p a b c d", b=shape[2], c=shape[3], d=shape[4])
            return a

        hTm = hTb.rearrange("p (k t) -> p k t", t=2048)
        hTf = hTb[:, 0:8192].rearrange("p (k t) -> p k t", t=1024)
        w13 = [hTb[:, 8192 + i * 2048: 8192 + (i + 1) * 2048].rearrange("p (k n) -> p k n", n=256) for i in range(4)]
        w1s, w3s = w13[0:2], w13[2:4]

        gT = carve(0, [128, NFC, 1024], BF16)
        w2sb = carve(45056, [128, NFC, 1024], BF16)
        sg = [carve(90112 + i * 2048, [128, 512], F32) for i in range(2)]
        xn_f = [carve(94208 + i * 2048, [128, 1024], BF16) for i in range(4)]
        mixedT = carve(0, [128, 8, 2048], BF16)
        A0 = 32768
        xn_m = [carve(A0 + i * 2048, [128, 1024], BF16) for i in range(2)]
        o = A0
        wqk = carve(o, [128, 8, 512], BF16); o += 8192
        wv = carve(o, [128, 8, 256], BF16); o += 4096
        qkT = carve(o, [128, 4, 2048], BF16); o += 16384
        Vaug = []
        for b in range(3):
            Vaug.append(carve(o, [128, 16, 2, 192], BF16)); o += 12288
        rden = carve(o, [128, 512], F32); o += 2048
        o += 2048
        pt = [carve(A0 + i * 1024, [128, 512], BF16) for i in range(6)]
        VA2 = A0 + 8192 + 4096 + 16384 + 2 * 12288
        qn2 = [carve(VA2 + i * 2048, [128, 512], F32) for i in range(2)]
        qr2 = [carve(VA2 + 4096 + i * 1024, [128, 512], BF16) for i in range(2)]
        rt2 = [carve(VA2 + 6144 + i * 1024, [128, 4, 64], F32) for i in range(2)]
        qn_a = carve(o, [128, 512], F32); o += 2048
        sq_a = qn_a
        qr_a = carve(o, [128, 512], BF16); o += 1024
        assert o <= ARENA_BYTES, o
        o = A0
        wi = carve(o, [128, 8, 512], BF16); o += 8192
        wqfg2 = []
        for i in range(2):
            wqfg2.append(carve(o, [128, 8, 3, 128], BF16)); o += 6144
        vtm = carve(o, [128, 16, 512], BF16); o += 16384
        hA, hB, hC = [], [], []
        for i in range(2):
            hA.append(carve(o, [128, 512], F32)); o += 2048
            hB.append(carve(o, [128, 512], F32)); o += 2048
            hC.append(carve(o, [128, 512], F32)); o += 2048
        qd, kd, kk, kktm, Sbf, silg = [], [], [], [], [], []
        for i in range(2):
            qd.append(carve(o, [128, 512], BF16)); o += 1024
            kd.append(carve(o, [128, 512], BF16)); o += 1024
            if i == 0:
                kk.append(carve(o, [128, 512], BF16)); o += 1024
            else:
                kk.append(kk[0])
            kktm.append(carve(o, [128, 4, 128], BF16)); o += 1024
            Sbf.append(carve(o, [128, 8, 128], BF16)); o += 2048
            silg.append(carve(o, [128, 512], BF16)); o += 1024
        silg.append(carve(o, [128, 512], BF16)); o += 1024
        hsq = carve(o, [128, 512], F32); o += 2048
        hrstd = carve(o, [128, 512], F32); o += 2048
        attm = []
        for i in range(4):
            attm.append(carve(o, [128, 128], BF16)); o += 256
        hscm = carve(o, [128, 512], F32); o += 2048
        hhi = carve(o, [128, 512], BF16); o += 1024
        hlo = carve(o, [128, 512], BF16); o += 1024
        assert o <= ARENA_BYTES, o
        woA = carve(A0, [128, 8, 512], BF16)
        woB = carve(A0 + 8192, [128, 8, 512], BF16)

        identb = cb[:, IDB:IDB + 128]
        amask = cb[:, MSK:MSK + 256]
        hmask = cb[:, HMK:HMK + 128]

        p = Prog(nc)
        PSK = lambda b, h=None: [("ps", b)]

        p.op("sp", lambda e: e.dma_start(out=cf[:, :], in_=cf_d[:, :]), writes=["cf"], dma=True)
        p.op("sp", lambda e: e.dma_start(out=cb[:, :], in_=cb_d[:, :]), writes=["cb"], dma=True)
        p.op("dve", lambda e: e.memset(ms[:, :], 0.0), writes=["ms_all"])
        p.op("dve", lambda e: e.tensor_tensor(out=lbt[:, 0:4], in0=cf[:, LBL:LBL + 4], in1=cf[:, LBL + 4:LBL + 8], op=ALU.subtract),
             reads=["cf"], writes=["lbt0"])
        p.op("dve", lambda e: e.tensor_tensor(out=lbt[:, 4:8], in0=cf[:, LBL + 4:LBL + 8], in1=cf[:, LBL:LBL + 4], op=ALU.subtract),
             reads=["cf"], writes=["lbt1"])
        p.op("act", lambda e: e.activation(out=lbt[:, :], in_=lbt[:, :], func=AF.Sigmoid), reads=["lbt0", "lbt1"], writes=["lbt"])

        nidx = [0]
        rot_T = Rot([0, 1])

        def norm_tile(tt, dst, dst_key, gidx, xn, defer=False):
            i = nidx[0]
            nidx[0] += 1
            slot = i % len(xn)
            xs = xn[slot]
            p.op("act", lambda e: e.activation(out=xs[:, :], in_=resid[:, tt, :], func=AF.Square, scale=1.0 / 32.0,
                                               accum_out=ms[:, i:i + 1]),
                 reads=[("res", tt), "ms_all"], writes=[("xn", slot), ("ms", i)])
            p.op("act", lambda e: e.activation(out=rs[:, i:i + 1], in_=ms[:, i:i + 1], func=AF.Sqrt, bias=EPS, scale=1.0),
                 reads=[("ms", i)], writes=[("rs", i)])
            p.op("dve", lambda e: e.reciprocal(out=rs[:, i:i + 1], in_=rs[:, i:i + 1]), reads=[("rs", i)], writes=[("rs", i)])
            p.op("dve", lambda e: e.tensor_scalar(out=xs[:, :], in0=resid[:, tt, :], scalar1=rs[:, i:i + 1], scalar2=None, op0=ALU.mult),
                 reads=[("res", tt), ("rs", i)], writes=[("xn", slot)])
            def part_b():
                tb = rot_T.next()
                for kc in range(8):
                    p.op("pe", lambda e, kc=kc: e.transpose(out=psb[tb][:, kc, :], in_=xs[:, kc * 128:(kc + 1) * 128], identity=identb),
                         reads=[("xn", slot), "cb"], writes=[("ps", 6 + tb)])
                gb = cf[:, GAM + gidx * 8:GAM + gidx * 8 + 8].unsqueeze(2).to_broadcast([128, 8, 128])
                p.op("dve", lambda e: e.tensor_tensor(out=dst, in0=psb[tb][:, :, :], in1=gb, op=ALU.mult),
                     reads=[("ps", 6 + tb), "cf"], writes=[dst_key])
            if defer:
                return part_b
            part_b()

        rot_h = Rot([0, 1, 2, 3])
        rot_o = Rot([(0, 1), (2, 3), (4, 5)])
        sgi = [0]

        def ffn(seq, half, w1d, w3d, w2d, gidx, store, do_norm=True, norm_next=False, mix_norm=False, next_seq_norm=False):
            tt0 = half * 8
            w1v, w3v = kv(w1d), kv(w3d)
            w2v = w2d.rearrange("(c p) d -> p c d", p=128)
            if do_norm:
                pq = []
                for t in range(8 + 3):
                    if t < 8:
                        pq.append(norm_tile(tt0 + t, hTf[:, :, t * 128:(t + 1) * 128], ("hTf", t), gidx, xn_f, defer=True))
                    if t >= 3:
                        pq.pop(0)()
            hkeys = [[("hTf", t) for t in range(th * 4, th * 4 + 4)] for th in range(2)]
            for cg in range(11):
                slot = cg % 2
                p.op("pool", lambda e, cg=cg, slot=slot: e.dma_start(out=w1s[slot][:, :, :], in_=w1v[:, :, cg * 256:(cg + 1) * 256]),
                     writes=[("w1b", slot)], dma=True)
                p.op("pool", lambda e, cg=cg, slot=slot: e.dma_start(out=w3s[slot][:, :, :], in_=w3v[:, :, cg * 256:(cg + 1) * 256]),
                     writes=[("w3b", slot)], dma=True)
                if cg in (1, 4):
                    pc = 0 if cg == 1 else 1
                    p.op("pool", lambda e, pc=pc: e.dma_start(out=w2sb[:, pc * 11:(pc + 1) * 11, :], in_=w2v[:, pc * 11:(pc + 1) * 11, :]),
                         writes=[("w2", pc)], dma=True)
                for cc in range(2):
                    c = cg * 2 + cc
                    for th in range(2):
                        b1 = rot_h.next()
                        b3 = rot_h.next()
                        for kc in range(8):
                            p.op("pe", lambda e, kc=kc, b1=b1, cc=cc, th=th, slot=slot: e.matmul(
                                psf[b1][:, :], lhsT=w1s[slot][:, kc, cc * 128:(cc + 1) * 128], rhs=hTf[:, kc, th * 512:(th + 1) * 512],
                                start=(kc == 0), stop=(kc == 7)),
                                reads=[("w1b", slot)] + hkeys[th], writes=PSK(b1))
                        for kc in range(8):
                            p.op("pe", lambda e, kc=kc, b3=b3, cc=cc, th=th, slot=slot: e.matmul(
                                psf[b3][:, :], lhsT=w3s[slot][:, kc, cc * 128:(cc + 1) * 128], rhs=hTf[:, kc, th * 512:(th + 1) * 512],
                                start=(kc == 0), stop=(kc == 7)),
                                reads=[("w3b", slot)] + hkeys[th], writes=PSK(b3))
                        si = sgi[0] % 2
                        sgi[0] += 1
                        p.op("act", lambda e, b1=b1, si=si: e.activation(out=sg[si][:, :], in_=psf[b1][:, :], func=AF.Silu),
                             reads=PSK(b1), writes=[("sg", si)])
                        p.op("dve", lambda e, b3=b3, si=si, c=c, th=th: e.tensor_tensor(
                            out=gT[:, c, th * 512:(th + 1) * 512], in0=sg[si][:, :], in1=psf[b3][:, :], op=ALU.mult),
                            reads=[("sg", si)] + PSK(b3), writes=[("gT", c, th)])
            if mix_norm:
                p.retire(["hTf", "w1b", "w3b"])
            due = {}
            for t in range(8):
                tt = tt0 + t
                for fb in due.pop(t, []):
                    fb()
                if norm_next:
                    due.setdefault(t + 1, []).append(norm_tile(tt0 + 8 + t, hTf[:, :, t * 128:(t + 1) * 128], ("hTf", t), gidx, xn_f, defer=True))
                if next_seq_norm:
                    due.setdefault(t + 1, []).append(norm_tile(t, hTf[:, :, t * 128:(t + 1) * 128], ("hTf", t), 0, xn_f, defer=True))
                if mix_norm:
                    due.setdefault(t + 1, []).append(norm_tile(t, hTm[:, :, t * 128:(t + 1) * 128], ("hTm", t), 1, xn_f, defer=True))
                banks = rot_o.next()
                for dh in range(2):
                    bk = banks[dh]
                    for c in range(NFC):
                        p.op("pe", lambda e, c=c, bk=bk, t=t, dh=dh: e.matmul(
                            psf[bk][:, :], lhsT=gT[:, c, t * 128:(t + 1) * 128], rhs=w2sb[:, c, dh * 512:(dh + 1) * 512],
                            start=(c == 0), stop=(c == NFC - 1)),
                            reads=[("gT", c, t // 4), ("w2", c // 11)], writes=PSK(bk))
                    p.op("dve", lambda e, bk=bk, tt=tt, dh=dh: e.scalar_tensor_tensor(
                        out=resid[:, tt, dh * 512:(dh + 1) * 512], in0=psf[bk][:, :], scalar=0.5,
                        in1=resid[:, tt, dh * 512:(dh + 1) * 512], op0=ALU.mult, op1=ALU.add),
                        reads=PSK(bk) + [("res", tt)], writes=[("res", tt)])
                if store:
                    g = seq * 16 + tt
                    p.op("sp", lambda e, g=g, tt=tt: e.dma_start(out=out_v[:, g, :], in_=resid[:, tt, :]),
                         reads=[("res", tt)], dma=True)
                if mix_norm:
                    tm = 8 + t
                    due.setdefault(t + 2, []).append(norm_tile(tm, hTm[:, :, tm * 128:(tm + 1) * 128], ("hTm", tm), 1, xn_f, defer=True))
            for k_ in sorted(due):
                for fb in due[k_]:
                    fb()

        FFN_NAMES = ["gT", "w2", "sg", "xn", "hTf", "w1b", "w3b"]
        ATT_NAMES = ["xn", "wqk", "wv", "qkT", "Va", "rden", "pt", "sq", "qn", "qr", "sqj", "rt"]
        HG_NAMES = ["wi", "wqfg", "vtm", "hA", "hB", "hC", "qd", "kd", "kk", "kktm", "Sbf", "silg", "hsq", "hrstd", "ht1", "attm", "hscm", "hhi", "hlo"]

        rot_full = Rot([0, 1, 2, 3, 4, 5])
        rot_half = Rot([(b, 0) for b in range(6)])
        rot_S = Rot([4, 5, 6, 7])
        pti = [0]
        evi = [0]

        def hslot(bh):
            b, h = bh
            return psf[b][:, h * 256:(h + 1) * 256]

        def attention_half(hf):
            allhT = [("hTm", tt) for tt in range(16)]
            def v_phase(bs):
              for b in bs:
                p.op("dve", lambda e, b=b: e.memset(Vaug[b][:, :, :, 64:128], 1.0), writes=[("Va", b, "ones")])
              for b in bs:
                Vv = Vaug[b].rearrange("p t a (k d) -> p t a k d", d=64)
                for tile in range(16):
                    if b == 0:
                        tok = lambda kc, tile=tile: hTm[:, kc, tile * 128:(tile + 1) * 128]
                        rk_ = [("hTm", tile)]
                    elif b == 1:
                        rho, j = tile // 4, tile % 4
                        s0 = 512 * j + rho
                        tok = lambda kc, s0=s0: hTm[:, kc, s0:s0 + 509:4]
                        rk_ = [("hTm", 4 * j + i) for i in range(4)]
                    else:
                        tok = lambda kc, tile=tile: hTm[:, kc, tile:tile + 2033:16]
                        rk_ = allhT
                    bh = rot_half.next()
                    for kc in range(8):
                        p.op("pe", lambda e, kc=kc, bh=bh, tok=tok: e.matmul(hslot(bh), lhsT=tok(kc), rhs=wv[:, kc, :], start=(kc == 0), stop=(kc == 7)),
                             reads=rk_ + [("wv", 0)], writes=PSK(*bh))
                    src = hslot(bh).rearrange("p (a k d) -> p a k d", a=2, k=2, d=64)
                    dst = Vv[:, tile, :, 0:3:2, :]
                    eng = "act" if evi[0] % 2 == 0 else "dve"
                    evi[0] += 1
                    if eng == "act":
                        p.op("act", lambda e, src=src, dst=dst: e.copy(out=dst, in_=src), reads=PSK(*bh), writes=[("Va", b, tile)])
                    else:
                        p.op("dve", lambda e, src=src, dst=dst: e.tensor_copy(out=dst, in_=src), reads=PSK(*bh), writes=[("Va", b, tile)])

            p.op("pool", lambda e: e.dma_start(out=wv[:, :, :], in_=win_v[:, :, 1024 + hf * 256:1024 + (hf + 1) * 256]), writes=[("wv", 0)], dma=True)
            p.op("pool", lambda e: e.dma_start(out=wqk[:, :, 0:256], in_=win_v[:, :, hf * 256:(hf + 1) * 256]), writes=[("wqk", 0)], dma=True)
            p.op("pool", lambda e: e.dma_start(out=wqk[:, :, 256:512], in_=win_v[:, :, 512 + hf * 256:512 + (hf + 1) * 256]), writes=[("wqk", 1)], dma=True)
            v_phase((0, 1))
            gq = cf[:, GQK:GQK + 512].rearrange("p (h d) -> p h d", d=64)
            qbank = {}

            def qk_mm(tt):
                bk = rot_full.next()
                qbank[tt] = bk
                for kc in range(8):
                    p.op("pe", lambda e, kc=kc: e.matmul(psf[bk][:, :], lhsT=hTm[:, kc, tt * 128:(tt + 1) * 128], rhs=wqk[:, kc, :],
                                                       start=(kc == 0), stop=(kc == 7)),
                         reads=[("hTm", tt), ("wqk", 0), ("wqk", 1)], writes=PSK(bk))

            def qk_Xa(tt):
                bk = qbank[tt]
                p.op("act", lambda e: e.activation(out=qn_a[:, :], in_=psf[bk][:, :], func=AF.Square, scale=0.125),
                     reads=PSK(bk), writes=["sqj"])

            def qk_Xb(tt):
                s4 = tt % 4
                p.op("dve", lambda e: e.tensor_reduce(out=ss8[:, s4, :], in_=qn_a.rearrange("p (h d) -> p h d", d=64), axis=AX.X, op=ALU.add),
                     reads=["sqj"], writes=[("ss8", s4)])
                p.op("act", lambda e: e.activation(out=rs8[:, s4, :], in_=ss8[:, s4, :], func=AF.Ln, bias=EPS, scale=1.0),
                     reads=[("ss8", s4)], writes=[("rs8", s4)])
                p.op("act", lambda e: e.activation(out=rs8[:, s4, :], in_=rs8[:, s4, :], func=AF.Exp, scale=-0.5),
                     reads=[("rs8", s4)], writes=[("rs8", s4)])

            def qk_Y(tt):
                bk = qbank[tt]
                sl = tt % 2
                qn_s, qr_s = qn2[sl], qr2[sl]
                ps3 = psf[bk].rearrange("p (h d) -> p h d", d=64)
                qn3 = qn_s.rearrange("p (h d) -> p h d", d=64)
                qr3 = qr_s.rearrange("p (h d) -> p h d", d=64)
                kqn = ("qn", sl)
                s4 = tt % 4
                p.op("dve", lambda e: e.tensor_tensor(out=qn3, in0=ps3, in1=rs8[:, s4, :].unsqueeze(2).to_broadcast([128, 8, 64]), op=ALU.mult),
                     reads=PSK(bk) + [("rs8", s4)], writes=[kqn])
                p.op("dve", lambda e: e.tensor_tensor(out=qn3, in0=qn3, in1=gq, op=ALU.mult), reads=[kqn, "cf"], writes=[kqn])
                off = (COS + tt * 24) if tt < 10 else (ONESF + (tt - 10) * 24)
                cs_ = cf[:, off:off + 16].unsqueeze(1).to_broadcast([128, 8, 16])
                sc_ = cf[:, off + 8:off + 24].unsqueeze(1).to_broadcast([128, 8, 16])
                x12 = qn3[:, :, 0:16]
                rtf = rt2[sl].rearrange("p a d -> p (a d)")
                TA = rtf[:, 0:128].rearrange("p (h d) -> p h d", d=16)
                TB = rtf[:, 128:256].rearrange("p (h d) -> p h d", d=16)
                rk = ("rt", sl)
                p.op("dve", lambda e: e.tensor_tensor(out=TA, in0=x12, in1=cs_, op=ALU.mult), reads=[kqn, "cf"], writes=[rk + (0,)])
                p.op("dve", lambda e: e.tensor_tensor(out=TB, in0=x12, in1=sc_, op=ALU.mult), reads=[kqn, "cf"], writes=[rk + (1,)])
                p.op("dve", lambda e: e.tensor_tensor(out=qr3[:, :, 0:8], in0=TA[:, :, 0:8], in1=TA[:, :, 8:16], op=ALU.subtract),
                     reads=[rk + (0,)], writes=[("qr", sl, 0)])
                p.op("dve", lambda e: e.tensor_tensor(out=qr3[:, :, 8:16], in0=TB[:, :, 0:8], in1=TB[:, :, 8:16], op=ALU.add),
                     reads=[rk + (1,)], writes=[("qr", sl, 1)])
                p.op("act", lambda e: e.copy(out=qr3[:, :, 16:64], in_=qn3[:, :, 16:64]), reads=[kqn], writes=[("qr", sl, 2)])
                tb = rot_T.next()
                for j in range(4):
                    p.op("pe", lambda e, j=j: e.transpose(out=psb[tb][:, j, :], in_=qr_s[:, j * 128:(j + 1) * 128], identity=identb),
                         reads=[("qr", sl, 0), ("qr", sl, 1), ("qr", sl, 2), "cb"], writes=[("ps", 6 + tb)])
                p.op("act", lambda e: e.copy(out=qkT[:, :, tt * 128:(tt + 1) * 128], in_=psb[tb][:, 0:4, :]),
                     reads=[("ps", 6 + tb)], writes=[("qkT", tt)])

            qk_mm(0)
            qk_mm(1)
            qk_mm(2)
            qk_Xa(0)
            qk_Xb(0)
            qk_Xa(1)
            qk_Xb(1)
            for tt in range(16):
                if tt + 3 < 16:
                    qk_mm(tt + 3)
                if tt + 2 < 16:
                    qk_Xa(tt + 2)
                qk_Y(tt)
                if tt + 2 < 16:
                    qk_Xb(tt + 2)
            p.retire(["qn", "qr", "rt", "sqj"])
            v_phase((2,))
            p.retire(["wqk", "wv", "qn", "qr"])
            allq = [("qkT", tt) for tt in range(16)]
            items = []
            for hh in range(4):
                h = 4 * hf + hh
                pair, odd = hh // 2, hh % 2
                base = 64 * odd
                qT = qkT[base:base + 64, pair, :]
                kT = qkT[base:base + 64, 2 + pair, :]
                va = lambda b, tile, pair=pair, odd=odd: Vaug[b][:, tile, pair, odd * 64:odd * 64 + 128]
                for g in range(8):
                    tiles = []
                    for j in (2 * g, 2 * g + 1):
                        nq = 256 if j < 15 else 128
                        pvs = []
                        for qb in range(nq // 128):
                            col0 = (j + qb) * 128
                            bank = col0 // 512
                            first = (j == 0 and qb == 0) or (qb == 1 and (j + 1) % 4 == 0)
                            pvs.append((bank, psf[bank][:, col0 % 512:col0 % 512 + 128], va(0, j), (qb * 128, (qb + 1) * 128), first, ("Va", 0, j)))
                        tiles.append((kT[:, j * 128:(j + 1) * 128], qT[:, j * 128:j * 128 + nq], nq,
                                      [("qkT", j)] + ([("qkT", j + 1)] if j < 15 else []), pvs))
                    items.append(("g", tiles))
                for rho in range(4):
                    for g in range(2):
                        tiles = []
                        for j in (2 * g, 2 * g + 1):
                            nq = 256 if j < 3 else 128
                            s0 = 512 * j + rho
                            pvs = []
                            for qb in range(nq // 128):
                                bank = j + qb
                                pvs.append((bank, psf[bank][:, rho:rho + 509:4], va(1, 4 * rho + j), (qb * 128, (qb + 1) * 128), False, ("Va", 1, 4 * rho + j)))
                            tiles.append((kT[:, s0:s0 + 509:4], qT[:, s0:s0 + 4 * (nq - 1) + 1:4], nq,
                                          [("qkT", 4 * j + i) for i in range(4 * (nq // 128))], pvs))
                        items.append(("g", tiles))
                for g in range(4):
                    tiles = []
                    for r in range(4 * g, 4 * g + 4):
                        pvs = [(bank, psf[bank][:, r:r + 497:16], va(2, r), (bank * 32, (bank + 1) * 32), False, ("Va", 2, r)) for bank in range(4)]
                        tiles.append((kT[:, r:r + 2033:16], qT[:, r:r + 2033:16], 128, allq, pvs))
                    items.append(("g", tiles))
                items.append(("fin", h, odd))

            def stage_S(tiles, state):
                sbk = rot_S.next()
                k = pti[0] % 6
                pti[0] += 1
                state["k"] = k
                pS = psf[sbk]
                col = 0
                offs = []
                for (kap, qap, nq, rkeys, pvs) in tiles:
                    p.op("pe", lambda e, kap=kap, qap=qap, col=col, nq=nq: e.matmul(pS[:, col:col + nq], lhsT=kap, rhs=qap, start=True, stop=True),
                         reads=rkeys, writes=PSK(sbk))
                    offs.append(col)
                    col += nq
                tot = col
                state["offs"] = offs
                p.op("act", lambda e: e.activation(out=pt[k][:, 0:tot], in_=pS[:, 0:tot], func=AF.Exp, scale=0.125),
                     reads=PSK(sbk), writes=[("pt", k)])
                nqs = [t[2] for t in tiles]
                if tot == 512 and len(set(nqs)) == 1:
                    n, w = len(nqs), nqs[0]
                    mv = cb[:, MSK:MSK + w].unsqueeze(1).to_broadcast([128, n, w])
                    pv_ = pt[k][:, :].rearrange("p (n w) -> p n w", w=w)
                    p.op("dve", lambda e: e.tensor_tensor(out=pv_, in0=pv_, in1=mv, op=ALU.mult), reads=[("pt", k), "cb"], writes=[("pt", k)])
                else:
                    for off, w in zip(offs, nqs):
                        p.op("dve", lambda e, off=off, w=w: e.tensor_tensor(out=pt[k][:, off:off + w], in0=pt[k][:, off:off + w],
                                                                          in1=cb[:, MSK:MSK + w], op=ALU.mult),
                             reads=[("pt", k), "cb"], writes=[("pt", k)])

            def stage_PV(tiles, state):
                k = state["k"]
                for (kap, qap, nq, rkeys, pvs), off in zip(tiles, state["offs"]):
                    for (bank, oap, lhs, (a, b_), first, vkey) in pvs:
                        p.op("pe", lambda e, oap=oap, lhs=lhs, a=a, b_=b_, off=off, first=first: e.matmul(
                            oap, lhsT=lhs, rhs=pt[k][:, off + a:off + b_], start=first, stop=False, skip_group_check=True),
                            reads=[vkey, ("pt", k), ("Va", vkey[1], "ones")], writes=PSK(bank))

            def stage_fin(h, odd):
                for bank in range(4):
                    if odd == 0:
                        nsl, dsl = slice(0, 64), slice(64, 128)
                    else:
                        nsl, dsl = slice(64, 128), slice(0, 64)
                    p.op("act", lambda e, bank=bank, nsl=nsl, dsl=dsl: e.activation(out=rden[nsl, :], in_=psf[bank][dsl, :], func=AF.Ln),
                         reads=PSK(bank), writes=["rden"])
                    p.op("act", lambda e, nsl=nsl: e.activation(out=rden[nsl, :], in_=rden[nsl, :], func=AF.Exp, scale=-1.0),
                         reads=["rden"], writes=["rden"])
                    p.op("dve", lambda e, bank=bank, nsl=nsl, h=h: e.tensor_tensor(
                        out=mixedT[nsl, h // 2, bank * 512:(bank + 1) * 512], in0=psf[bank][nsl, :], in1=rden[nsl, :], op=ALU.mult),
                        reads=PSK(bank) + ["rden"], writes=[("mixedT", h, bank)])

            L = 3
            states = [dict() for _ in items]
            for idx in range(len(items) + L):
                if idx < len(items) and items[idx][0] == "g":
                    stage_S(items[idx][1], states[idx])
                if idx >= L:
                    it = items[idx - L]
                    if it[0] == "g":
                        stage_PV(it[1], states[idx - L])
                    else:
                        stage_fin(it[1], it[2])

        hbi = [0]
        ami = [0]
        rot_hb = Rot([5, 7])
        hgc = [0]

        def hgrn():
            p.op("pool", lambda e: e.dma_start(out=wi[:, :, :], in_=win_v[:, :, 2560:3072]), writes=[("wi", 0)], dma=True)
            for tt in range(16):
                bk = rot_full.next()
                for kc in range(8):
                    p.op("pe", lambda e, kc=kc, bk=bk, tt=tt: e.matmul(psf[bk][:, :], lhsT=hTm[:, kc, tt * 128:(tt + 1) * 128], rhs=wi[:, kc, :],
                                                                     start=(kc == 0), stop=(kc == 7)),
                         reads=[("hTm", tt), ("wi", 0)], writes=PSK(bk))
                if tt % 2 == 0:
                    p.op("act", lambda e, bk=bk, tt=tt: e.copy(out=vtm[:, tt, :], in_=psf[bk][:, :]), reads=PSK(bk), writes=[("vtm", tt)])
                else:
                    p.op("dve", lambda e, bk=bk, tt=tt: e.tensor_copy(out=vtm[:, tt, :], in_=psf[bk][:, :]), reads=PSK(bk), writes=[("vtm", tt)])
            p.op("dve", lambda e: e.memset(hscm[:, :], 1.0), writes=["hscm"])
            p.op("dve", lambda e: e.memset(hscm[:, 0:512:64], 0.0), writes=["hscm"])
            blocks = [(h, cbk) for h in range(4) for cbk in range(4)]
            sts = [dict() for _ in blocks]
            for bi_, st_ in enumerate(sts):
                st_["sl"] = hbi[0] % 2
                st_["sg"] = bi_ % 3
                st_["bo"] = 3 + (bi_ % 2)
                hbi[0] += 1
            def interleave(gens):
                live = [[g, p.eng_free["pe"] * 0.0] for (g, w) in gens]
                while live:
                    it = min(live, key=lambda x: x[1])
                    try:
                        next(it[0])
                        it[1] = p.last_end
                    except StopIteration:
                        live.remove(it)

            nb_ = len(blocks)
            interleave([(hg_front(0, 0, sts[0]), 1)])
            p.retire(["wi"])
            p.op("pool", lambda e: e.dma_start(out=woA[:, :, :], in_=wout_v[:, :, 0:512]), writes=[("woA", 0)], dma=True)
            for i in range(nb_ + 1):
                gens = []
                if i + 1 < nb_:
                    gens.append((hg_front(blocks[i + 1][0], blocks[i + 1][1], sts[i + 1]), 2))
                if i < nb_:
                    gens.append((hg_back(blocks[i][0], blocks[i][1], sts[i]), 3))
                if i >= 1:
                    gens.append((hg_out(blocks[i - 1][0], blocks[i - 1][1], sts[i - 1]), 1))
                interleave(gens)

        def hg_front(h, cbk, stt):
            wsl = h % 2
            if cbk == 0:
                for i, c0 in enumerate((1536, 2048, 3072)):
                    p.op("pool", lambda e, i=i, c0=c0: e.dma_start(out=wqfg2[wsl][:, :, i, :], in_=win_v[:, :, c0 + h * 128:c0 + (h + 1) * 128]),
                         writes=[("wqfg", wsl, i)], dma=True)
                    yield
            lb_h = lbt[:, h:h + 1]
            oml_h = lbt[:, 4 + h:5 + h]
            sl = stt["sl"]
            sgs = stt["sg"]
            cols = slice(cbk * 512, (cbk + 1) * 512)
            hkeys = [("hTm", cbk * 4 + i) for i in range(4)]
            bq, bf_, bg = 0, 1, 2
            for i, bk in ((1, bf_), (2, bg), (0, bq)):
                for kc in range(8):
                    p.op("pe", lambda e, kc=kc, bk=bk, i=i: e.matmul(psf[bk][:, :], lhsT=wqfg2[wsl][:, kc, i, :], rhs=hTm[:, kc, cols],
                                                                   start=(kc == 0), stop=(kc == 7)),
                         reads=hkeys + [("wqfg", wsl, i)], writes=PSK(bk), dur=0.3)
                yield
            A, B, C = hA[sl], hB[sl], hC[sl]
            p.op("act", lambda e: e.activation(out=A[:, :], in_=psf[bf_][:, :], func=AF.Sigmoid), reads=PSK(bf_), writes=[("hA", sl)], aset="sig")
            yield
            p.op("act", lambda e: e.activation(out=B[:, :], in_=psf[bf_][:, :], func=AF.Sigmoid, scale=-1.0), reads=PSK(bf_), writes=[("hB", sl)], aset="sig")
            yield
            p.op("act", lambda e: e.activation(out=silg[sgs][:, :], in_=psf[bg][:, :], func=AF.Sigmoid), reads=PSK(bg), writes=[("silg", sgs)], aset="sig")
            yield
            p.op("dve", lambda e: e.tensor_tensor(out=silg[sgs][:, :], in0=psf[bg][:, :], in1=silg[sgs][:, :], op=ALU.mult),
                 reads=PSK(bg) + [("silg", sgs)], writes=[("silg", sgs)])
            yield
            p.op("act", lambda e: e.activation(out=A[:, :], in_=A[:, :], func=AF.Ln, scale=oml_h, bias=lb_h),
                 reads=[("hA", sl), "lbt"], writes=[("hA", sl)], aset="le")
            yield
            p.op("dve", lambda e: e.tensor_tensor_scan(out=C[:, :], data0=hscm[:, :], data1=A[:, :], initial=0.0, op0=ALU.mult, op1=ALU.add),
                 reads=[("hA", sl), "hscm"], writes=[("hC", sl)], dur=1.25)
            yield
            p.op("act", lambda e: e.activation(out=A[:, :], in_=C[:, :], func=AF.Exp), reads=[("hC", sl)], writes=[("hA", sl)], aset="le")
            yield
            p.op("act", lambda e: e.activation(out=C[:, :], in_=C[:, :], func=AF.Exp, scale=-1.0), reads=[("hC", sl)], writes=[("hC", sl)], aset="le")
            yield
            p.op("dve", lambda e: e.tensor_tensor(out=qd[sl][:, :], in0=psf[bq][:, :], in1=A[:, :], op=ALU.mult),
                 reads=PSK(bq) + [("hA", sl)], writes=[("qd", sl)])
            yield
            p.op("dve", lambda e: e.scalar_tensor_tensor(out=kd[sl][:, :], in0=B[:, :], scalar=oml_h, in1=C[:, :], op0=ALU.mult, op1=ALU.mult),
                 reads=[("hB", sl), ("hC", sl), "lbt"], writes=[("kd", sl)])
            yield
            dec = A[:, 63:512:64]
            p.op("dve", lambda e: e.tensor_tensor(out=kk[sl].rearrange("p (c t) -> p c t", t=64),
                                                  in0=kd[sl].rearrange("p (c t) -> p c t", t=64),
                                                  in1=dec.unsqueeze(2).to_broadcast([128, 8, 64]), op=ALU.mult),
                 reads=[("kd", sl), ("hA", sl)], writes=[("kk", 0)])
            yield
            tb = 0
            for i in range(4):
                p.op("pe", lambda e, i=i: e.transpose(out=psb[tb][:, i, :], in_=kk[sl][:, i * 128:(i + 1) * 128], identity=identb),
                     reads=[("kk", 0), "cb"], writes=[("ps", 6 + tb)])
            p.op("act", lambda e: e.copy(out=kktm[sl][:, :, :], in_=psb[tb][:, 0:4, :]), reads=[("ps", 6 + tb)], writes=[("kktm", sl)])
            yield

        def hg_back(h, cbk, stt):
            sl = stt["sl"]
            A = hA[sl]
            cols = slice(cbk * 512, (cbk + 1) * 512)
            if cbk == 0:
                p.op("dve", lambda e: e.memset(Sf[:, 0, :], 0.0), writes=[("Sf", 0)])
                yield
                p.op("dve", lambda e: e.memset(Sbf[sl][:, 0, :], 0.0), writes=[("Sbf", sl, 0)])
                yield
                hgc[0] = 0
            bo = stt["bo"]
            pend = None
            for i in range(4):
                tt = cbk * 4 + i
                ba = rot_hb.next()
                while ba == bo:
                    ba = rot_hb.next()
                am = ami[0] % 4
                ami[0] += 1
                p.op("pe", lambda e, i=i, ba=ba: e.matmul(psf[ba][:, 0:128], lhsT=kd[sl][:, i * 128:(i + 1) * 128], rhs=qd[sl][:, i * 128:(i + 1) * 128],
                                                         start=True, stop=True),
                     reads=[("kd", sl), ("qd", sl)], writes=PSK(ba))
                p.op("dve", lambda e, ba=ba, am=am: e.tensor_tensor(out=attm[am][:, :], in0=psf[ba][:, 0:128], in1=hmask, op=ALU.mult),
                     reads=PSK(ba) + ["cb"], writes=[("attm", am)], dur=0.3)
                yield
                for cc in range(2):
                    c = i * 2 + cc
                    bc = rot_hb.next()
                    while bc == bo:
                        bc = rot_hb.next()
                    p.op("pe", lambda e, i=i, cc=cc, bc=bc, tt=tt: e.matmul(
                        psf[bc][:, 0:128], lhsT=kktm[sl][cc * 64:(cc + 1) * 64, i, :], rhs=vtm[cc * 64:(cc + 1) * 64, tt, h * 128:(h + 1) * 128],
                        start=True, stop=True),
                        reads=[("kktm", sl), ("vtm", tt)], writes=PSK(bc))
                    so, sn = hgc[0] % 2, (hgc[0] + 1) % 2
                    hgc[0] += 1
                    p.op("dve", lambda e, so=so, sn=sn, c=c, bc=bc: e.scalar_tensor_tensor(
                        out=Sf[:, sn, :], in0=Sf[:, so, :], scalar=A[:, c * 64 + 63:c * 64 + 64], in1=psf[bc][:, 0:128],
                        op0=ALU.mult, op1=ALU.add),
                        reads=[("Sf", so), ("hA", sl)] + PSK(bc), writes=[("Sf", sn)], dur=0.35)
                    yield
                    if c < 7:
                        dkey = ("Sbf", sl, c + 1)
                        dap = Sbf[sl][:, c + 1, :]
                    else:
                        dkey = ("Sbf", 1 - sl, 0)
                        dap = Sbf[1 - sl][:, 0, :]
                    p.op("act", lambda e, sn=sn, dap=dap: e.copy(out=dap, in_=Sf[:, sn, :]), reads=[("Sf", sn)], writes=[dkey], dur=0.3)
                    yield

                def emit_po(i=i, tt=tt, am=am):
                    oap = psf[bo][:, i * 128:(i + 1) * 128]
                    okey = PSK(bo)
                    p.op("pe", lambda e: e.matmul(oap, lhsT=vtm[:, tt, h * 128:(h + 1) * 128], rhs=attm[am][:, :], start=True, stop=False,
                                                  skip_group_check=True),
                         reads=[("vtm", tt), ("attm", am)], writes=okey)
                    for cc in range(2):
                        c = i * 2 + cc
                        p.op("pe", lambda e, cc=cc, c=c: e.matmul(oap[:, cc * 64:(cc + 1) * 64], lhsT=Sbf[sl][:, c, :],
                                                                 rhs=qd[sl][:, i * 128 + cc * 64:i * 128 + (cc + 1) * 64],
                                                                 start=False, stop=(cc == 1), skip_group_check=True),
                             reads=[("Sbf", sl, c), ("qd", sl)], writes=okey)
                    yield
                if pend is not None:
                    yield from pend()
                pend = emit_po
            yield from pend()

        def hg_out(h, cbk, stt):
            sl = stt["sl"]
            sgs = stt["sg"]
            bo = stt["bo"]
            cols = slice(cbk * 512, (cbk + 1) * 512)
            p.op("act", lambda e: e.activation(out=hsq[:, :], in_=psf[bo][:, :], func=AF.Square), reads=PSK(bo), writes=["hsq"])
            yield
            bs = rot_hb.next()
            while bs == bo:
                bs = rot_hb.next()
            p.op("act", lambda e: e.copy(out=hhi[:, :], in_=hsq[:, :]), reads=["hsq"], writes=["hhi"])
            yield
            p.op("dve", lambda e: e.tensor_tensor(out=hlo[:, :], in0=hsq[:, :], in1=hhi[:, :], op=ALU.subtract),
                 reads=["hsq", "hhi"], writes=["hlo"])
            yield
            p.op("pe", lambda e: e.matmul(psf[bs][:, :], lhsT=cb[:, ONEB:ONEB + 128], rhs=hhi[:, :], start=True, stop=False),
                 reads=["hhi", "cb"], writes=PSK(bs))
            p.op("pe", lambda e: e.matmul(psf[bs][:, :], lhsT=cb[:, ONEB:ONEB + 128], rhs=hlo[:, :], start=False, stop=True),
                 reads=["hlo", "cb"], writes=PSK(bs))
            p.op("act", lambda e: e.activation(out=hrstd[:, :], in_=psf[bs][:, :], func=AF.Ln, bias=EPS, scale=1.0),
                 reads=PSK(bs), writes=["hrstd"], aset="le")
            yield
            p.op("act", lambda e: e.activation(out=hrstd[:, :], in_=hrstd[:, :], func=AF.Exp, scale=-0.5), reads=["hrstd"], writes=["hrstd"], aset="le")
            yield
            p.op("dve", lambda e: e.scalar_tensor_tensor(out=hsq[:, :], in0=psf[bo][:, :], scalar=cf[:, GOUT:GOUT + 1], in1=hrstd[:, :],
                                                         op0=ALU.mult, op1=ALU.mult),
                 reads=PSK(bo) + ["hrstd", "cf", "hhi", "hlo"], writes=["hsq"])
            yield
            p.op("dve", lambda e: e.tensor_tensor(out=mixedT[:, 4 + h, cols], in0=hsq[:, :], in1=silg[sgs][:, :], op=ALU.mult),
                 reads=["hsq", ("silg", sgs)], writes=[("mixedT", 8 + h, cbk)])
            yield

        def wout(ffn_norm=False):
            p.op("pool", lambda e: e.dma_start(out=woB[:, :, :], in_=wout_v[:, :, 512:1024]), writes=[("woB", 0)], dma=True)
            mkeys = [("mixedT", h, b) for h in range(8) for b in range(4)] + [("mixedT", 8 + h, b) for h in range(4) for b in range(4)]
            due = {}
            for dh, (wsb, wkey) in enumerate(((woA, ("woA", 0)), (woB, ("woB", 0)))):
                for tt in range(16):
                    g_ = dh * 16 + tt
                    for fb in due.pop(g_, []):
                        fb()
                    bk = rot_full.next()
                    for kc in range(8):
                        p.op("pe", lambda e, kc=kc, bk=bk, tt=tt, wsb=wsb: e.matmul(
                            psf[bk][:, :], lhsT=mixedT[:, kc, tt * 128:(tt + 1) * 128], rhs=wsb[:, kc, :],
                            start=(kc == 0), stop=(kc == 7)),
                            reads=mkeys + [wkey], writes=PSK(bk))
                    p.op("dve", lambda e, bk=bk, tt=tt, dh=dh: e.tensor_tensor(
                        out=resid[:, tt, dh * 512:(dh + 1) * 512], in0=psf[bk][:, :], in1=resid[:, tt, dh * 512:(dh + 1) * 512], op=ALU.add),
                        reads=PSK(bk) + [("res", tt)], writes=[("res", tt)])
                    if ffn_norm and dh == 1 and tt < 8:
                        due.setdefault(g_ + 2, []).append(norm_tile(tt, hTf[:, :, tt * 128:(tt + 1) * 128], ("hTf", tt), 2, xn_f, defer=True))
            for k_ in sorted(due):
                for fb in due[k_]:
                    fb()

        def load_x(seq, g0, g1):
            for g in range(g0, g1):
                p.op("sp", lambda e, g=g, seq=seq: e.dma_start(out=resid[:, 2 * g:2 * g + 2, :], in_=x_v[:, seq * 16 + 2 * g:seq * 16 + 2 * g + 2, :]),
                     writes=[("res", 2 * g), ("res", 2 * g + 1)], dma=True)

        chain = (stage >= 3)
        for seq in range(nseq):
            first = (seq == 0) or not chain
            if first:
                load_x(seq, 0, 8)
            ffn(seq, 0, w1a, w3a, w2a, 0, store=(stage == 1), do_norm=first, norm_next=True)
            ffn(seq, 1, w1a, w3a, w2a, 0, store=(stage == 1), do_norm=False, mix_norm=(stage >= 2))
            p.retire(FFN_NAMES)
            if stage >= 2:
                if dbg and seq == 0:
                    p.op("sp", lambda e: e.dma_start(out=dbg_hT[:, :, :], in_=hTm[:, :, :]), reads=[("hTm", tt) for tt in range(16)], dma=True)
                p.retire(["xn"])
                for hf in range(2):
                    if os.environ.get("SKIP_ATT"):
                        continue
                    attention_half(hf)
                    p.retire(ATT_NAMES)
                if not os.environ.get("SKIP_HG"):
                    hgrn()
                p.retire(HG_NAMES + ["hTm"])
                if dbg and seq == 0:
                    p.op("sp", lambda e: e.dma_start(out=dbg_mixed[:, :, :], in_=mixedT[:, :, :]),
                         reads=[("mixedT", h, b) for h in range(12) for b in range(4)], dma=True)
                wout(ffn_norm=(stage >= 3))
                p.retire(["woA", "woB", "mixedT"])
                if stage == 2:
                    for tt in range(16):
                        p.op("sp", lambda e, tt=tt, seq=seq: e.dma_start(out=out_v[:, seq * 16 + tt, :], in_=resid[:, tt, :]),
                             reads=[("res", tt)], dma=True)
            if stage >= 3:
                ffn(seq, 0, w1b_, w3b_, w2b_, 2, store=True, do_norm=False, norm_next=True)
                more = chain and seq + 1 < nseq
                if more:
                    load_x(seq + 1, 0, 4)
                ffn(seq, 1, w1b_, w3b_, w2b_, 2, store=True, do_norm=False, next_seq_norm=more)
                if more:
                    load_x(seq + 1, 4, 8)
                else:
                    p.retire(FFN_NAMES)
            p.retire(["hTm"])
        p.emit(st)
        build.stats = {e: len([o for o in p.ops if o.eng == e]) for e in ENGS}
        build.maxcnt = p.max_cnt
    return nc


def make_consts(inputs):
    f32 = np.float32
    cf = np.zeros((128, NCF), f32)
    pidx = np.arange(128)
    for n, name in enumerate(("ffn1_norm", "mix_norm", "ffn2_norm")):
        g = np.asarray(inputs[name], f32).reshape(8, 128)
        cf[:, GAM + n * 8:GAM + n * 8 + 8] = g.T
    inv = (f32(500000.0) ** (-(np.arange(0, 16, 2, dtype=f32)) / f32(16))).astype(f32)
    pos = (np.arange(16)[None, :] * 128 + pidx[:, None]).astype(f32)
    ang = (pos[:, :, None] * inv[None, None, :]).astype(f32)
    c_ = np.cos(ang).astype(f32)
    s_r = np.sin(ang).astype(f32)
    csc = np.concatenate([c_, s_r, c_], axis=2)
    cf[:, COS:COS + 240] = csc[:, 0:10, :].reshape(128, 240)
    cf[:, ONESF:ONESF + 144] = csc[:, 10:16, :].reshape(128, 144)
    qn = np.asarray(inputs["q_norm"], f32).reshape(64)
    kn = np.asarray(inputs["k_norm"], f32).reshape(64)
    cf[:, GQK:GQK + 512] = np.concatenate([np.tile(qn, 4), np.tile(kn, 4)])[None, :]
    lbl = np.asarray(inputs["hg_lb_logits"], f32).reshape(2, 4, 128)
    cf[:, LBL:LBL + 8] = lbl.transpose(2, 0, 1).reshape(128, 8)
    cf[:, GOUT] = np.asarray(inputs["hg_out_norm"], f32).reshape(128)
    scm = np.ones(64, f32)
    scm[0] = 0.0
    cb = np.zeros((128, NCB), f32)
    cb[:, IDB:IDB + 128] = np.eye(128, dtype=f32)
    cb[:, ONEB:ONEB + 128] = 1.0 / 128.0
    s_ = pidx[:, None]
    t_ = np.arange(256)[None, :]
    cb[:, MSK:MSK + 256] = ((t_ - s_ >= 0) & (t_ - s_ <= 128)).astype(f32)
    t2 = np.arange(128)[None, :]
    cb[:, HMK:HMK + 128] = ((s_ // 64 == t2 // 64) & (s_ <= t2)).astype(f32)
    return cf, cb.astype(ml_dtypes.bfloat16)


_NC_CACHE = {}


def run(inputs, stage=3, dbg=False, cores=NCORES, trace=False):
    key = (stage, dbg)
    if key not in _NC_CACHE:
        _NC_CACHE[key] = build(stage, dbg)
    nc = _NC_CACHE[key]
    cf, cb = make_consts(inputs)
    x = np.ascontiguousarray(np.asarray(inputs["x"], np.float32)).reshape(NCORES, TOK, D)
    shared = {
        "cf": cf, "cb": cb,
        "w_in": np.ascontiguousarray(np.asarray(inputs["w_in"], np.float32)[0]),
        "w_out": np.ascontiguousarray(np.asarray(inputs["w_out"], np.float32)[0]),
    }
    for n in ("ffn1_w1", "ffn1_w3", "ffn1_w2", "ffn2_w1", "ffn2_w3", "ffn2_w2"):
        shared[n] = np.ascontiguousarray(np.asarray(inputs[n], np.float32)[0])
    in_maps = [dict(shared, x=x[i]) for i in range(cores)]
    res = run_bass_kernel_spmd(nc, in_maps, core_ids=list(range(cores)), **({"trace": True} if trace else {}))
    return res


def kernel(**inputs):
    res = run(inputs)
    out = np.stack([np.asarray(r["out"], np.float32) for r in res.results], axis=0)
    return out.reshape(16, S, D)
```
